# Optimizing a Trainium2 kernel written in Bass

```python
import math
import jax, jax.numpy as jnp
from jax import lax
import numpy as np


D_MODEL = 1024
BATCH = 16
SEQ = 2048
DEPTH = 4

N_A_LAYERS = DEPTH // 2
N_B_LAYERS = DEPTH - N_A_LAYERS
GLA_HEADS = 4
GLA_DK = D_MODEL // 2 // GLA_HEADS
GLA_DV = D_MODEL // GLA_HEADS
GLA_KEY_WIDTH = GLA_HEADS * GLA_DK
GLA_VAL_WIDTH = GLA_HEADS * GLA_DV
GATE_RANK = 16
GATE_TAU = 16.0
GLA_CHUNK = 64
GLA_IN_WIDTH = 2 * GLA_KEY_WIDTH + GLA_VAL_WIDTH + GATE_RANK + GLA_VAL_WIDTH
DIL_GROUPS = ((128, 1), (512, 4), (2048, 16))
N_GROUPS = len(DIL_GROUPS)
B_HEADS = 8
B_HEAD_DIM = 128
B_WIDTH = B_HEADS * B_HEAD_DIM
ROT_DIM = B_HEAD_DIM // 4
ROPE_THETA = 500000.0
BAND_BLOCK = 128
D_FF = 2816
CONV_WIDTH = 3
NORM_EPS = 1e-6
NEG_INF = -1e30

kernel_name = 'yoco_gla_dilated_convffn'


def rms_norm(x, gain):
    xf = x.astype(jnp.float32)
    y = xf * lax.rsqrt(jnp.mean(xf * xf, axis=-1, keepdims=True) + NORM_EPS)
    return (y * gain.astype(jnp.float32)).astype(x.dtype)


def partial_rotary(x, positions):
    half = ROT_DIM // 2
    inv_freq = ROPE_THETA ** (-(jnp.arange(half, dtype=jnp.float32) * 2.0 / ROT_DIM))
    ang = positions.astype(jnp.float32)[..., None] * inv_freq
    ang = ang.reshape(ang.shape[:2] + (1,) * (x.ndim - 3) + (half,))
    cos, sin = jnp.cos(ang), jnp.sin(ang)
    xf = x.astype(jnp.float32)
    x1, x2, rest = xf[..., :half], xf[..., half:ROT_DIM], xf[..., ROT_DIM:]
    out = jnp.concatenate([x1 * cos - x2 * sin, x2 * cos + x1 * sin, rest], axis=-1)
    return out.astype(x.dtype)


def gla_mixer(h, w_in, w_gate, b_gate, out_norm, w_out):
    B_, S_, _ = h.shape
    proj = h @ w_in
    q, k, v, g_lr, r = jnp.split(
        proj, [GLA_KEY_WIDTH, 2 * GLA_KEY_WIDTH, 2 * GLA_KEY_WIDTH + GLA_VAL_WIDTH,
               2 * GLA_KEY_WIDTH + GLA_VAL_WIDTH + GATE_RANK], axis=-1)
    logit = (g_lr @ w_gate + b_gate).astype(jnp.float32)
    log_a = jax.nn.log_sigmoid(logit) / GATE_TAU
    n = S_ // GLA_CHUNK

    def chunks(t, d):
        return t.astype(jnp.float32).reshape(B_, n, GLA_CHUNK, GLA_HEADS, d)

    q = chunks(q, GLA_DK) * (GLA_DK ** -0.5)
    k = chunks(k, GLA_DK)
    v = chunks(v, GLA_DV)
    b = jnp.cumsum(chunks(log_a, GLA_DK), axis=2)
    b_last = b[:, :, -1:]
    q_t = q * jnp.exp(b)
    k_t = k * jnp.exp(-b)
    causal = jnp.tril(jnp.ones((GLA_CHUNK, GLA_CHUNK), dtype=bool))
    att = jnp.where(causal, jnp.einsum('bncha,bnmha->bnhcm', q_t, k_t), 0.0)
    o_intra = jnp.einsum('bnhcm,bnmhv->bnchv', att, v)
    u = jnp.einsum('bncha,bnchv->bnhav', k * jnp.exp(b_last - b), v)
    decay = jnp.exp(b_last[:, :, 0])

    def step(state, xs):
        d, uc = xs
        return d[..., None] * state + uc, state

    init = jnp.zeros((B_, GLA_HEADS, GLA_DK, GLA_DV), jnp.float32)
    _, states = lax.scan(step, init, (jnp.moveaxis(decay, 1, 0), jnp.moveaxis(u, 1, 0)))
    o_inter = jnp.einsum('bncha,nbhav->bnchv', q_t, states)
    o = (o_intra + o_inter).reshape(B_, S_, GLA_HEADS, GLA_DV)
    o = rms_norm(o, out_norm).reshape(B_, S_, GLA_VAL_WIDTH).astype(h.dtype)
    return (o * jax.nn.silu(r)) @ w_out


def dilated_window_attention(q, k, v, window, dilation):
    B_, S_, H_, hd = q.shape
    L = S_ // dilation
    w = window // dilation
    C = BAND_BLOCK
    nb = -(-L // C)
    Lp = nb * C

    def to_sub(t):
        return t.reshape(B_, L, dilation, H_, hd).transpose(0, 2, 1, 3, 4)

    qs = jnp.pad(to_sub(q), ((0, 0), (0, 0), (0, Lp - L), (0, 0), (0, 0))).reshape(B_, dilation, nb, C, H_, hd)

    def key_bands(t):
        tp = jnp.pad(to_sub(t), ((0, 0), (0, 0), (C, Lp - L), (0, 0), (0, 0)))
        tp = tp.reshape(B_, dilation, nb + 1, C, H_, hd)
        return jnp.concatenate([tp[:, :, :-1], tp[:, :, 1:]], axis=3)

    kb, vb = key_bands(k), key_bands(v)
    s = jnp.einsum('bgnqhd,bgnkhd->bgnhqk', qs, kb,
                   preferred_element_type=jnp.float32) * (hd ** -0.5)
    qi = jnp.arange(C)[:, None]
    ki = jnp.arange(2 * C)[None, :]
    dist = qi + C - ki
    key_pos = jnp.arange(nb)[:, None, None] * C - C + ki[None]
    mask = (dist >= 0)[None] & (dist <= w)[None] & (key_pos >= 0)
    s = jnp.where(mask[:, None], s, NEG_INF)
    lse = jax.nn.logsumexp(s, axis=-1)
    p = jnp.exp(s - lse[..., None])
    o = jnp.einsum('bgnhqk,bgnkhd->bgnqhd', p.astype(v.dtype), vb)
    o = o.reshape(B_, dilation, Lp, H_, hd)[:, :, :L].transpose(0, 2, 1, 3, 4).reshape(B_, S_, H_, hd)
    lse = lse.transpose(0, 1, 2, 4, 3).reshape(B_, dilation, Lp, H_)[:, :, :L]
    lse = lse.transpose(0, 2, 1, 3).reshape(B_, S_, H_)
    return o, lse


def shared_kv(x, kv_norm, w_kv, k_norm, positions):
    B_, S_, _ = x.shape
    kv = (rms_norm(x, kv_norm) @ w_kv).reshape(B_, S_, 2, N_GROUPS, B_HEADS, B_HEAD_DIM)
    k = partial_rotary(rms_norm(kv[:, :, 0], k_norm[:, None, :]), positions)
    return k, kv[:, :, 1]


def dilated_mixer(h, w_q, q_norm, w_out, k_sh, v_sh, positions):
    B_, S_, _ = h.shape
    q = (h @ w_q).reshape(B_, S_, N_GROUPS, B_HEADS, B_HEAD_DIM)
    q = partial_rotary(rms_norm(q, q_norm[:, None, :]), positions)
    outs, lses = [], []
    for g, (window, dilation) in enumerate(DIL_GROUPS):
        o_g, lse_g = dilated_window_attention(q[:, :, g], k_sh[:, :, g], v_sh[:, :, g], window, dilation)
        outs.append(o_g)
        lses.append(lse_g)
    wts = jax.nn.softmax(jnp.stack(lses, axis=0), axis=0)
    o = jnp.sum(wts[..., None] * jnp.stack(outs, axis=0).astype(jnp.float32), axis=0)
    return o.reshape(B_, S_, B_WIDTH).astype(h.dtype) @ w_out


def conv_ffn(h, w_up, conv_w, conv_b, w_down):
    u = h @ w_up
    u = lax.conv_general_dilated(u, conv_w[:, None, :].astype(u.dtype), window_strides=(1,),
                                 padding=[(CONV_WIDTH - 1, 0)],
                                 dimension_numbers=('NWC', 'WIO', 'NWC'),
                                 feature_group_count=u.shape[-1]) + conv_b
    gate, val = jnp.split(u, 2, axis=-1)
    return (jax.nn.silu(gate) * val) @ w_down


def setup_inputs(seed: int = 0) -> dict:
    key = jax.random.key(seed)
    ks = jax.random.split(key, 20)
    f32 = jnp.float32
    out_scale = (2 * DEPTH) ** -0.5

    def nrm(k, shape, scale):
        return jax.random.normal(k, shape, f32) * scale

    def gain(k, shape):
        return 1.0 + 0.05 * jax.random.normal(k, shape, f32)

    x = jax.random.normal(ks[0], (BATCH, SEQ, D_MODEL), f32)
    positions = (jax.random.randint(ks[1], (BATCH, 1), 0, 4096, dtype=jnp.int32)
                 + jnp.arange(SEQ, dtype=jnp.int32)[None, :])
    return {
        'x': x,
        'positions': positions,
        'a_norm': gain(ks[2], (N_A_LAYERS, D_MODEL)),
        'a_w_in': nrm(ks[3], (N_A_LAYERS, D_MODEL, GLA_IN_WIDTH), D_MODEL ** -0.5),
        'a_w_gate': nrm(ks[4], (N_A_LAYERS, GATE_RANK, GLA_KEY_WIDTH), GATE_RANK ** -0.5),
        'a_b_gate': nrm(ks[5], (N_A_LAYERS, GLA_KEY_WIDTH), 0.1),
        'a_out_norm': gain(ks[6], (N_A_LAYERS, GLA_DV)),
        'a_w_out': nrm(ks[7], (N_A_LAYERS, GLA_VAL_WIDTH, D_MODEL), GLA_VAL_WIDTH ** -0.5 * out_scale),
        'kv_norm': gain(ks[8], (D_MODEL,)),
        'w_kv': nrm(ks[9], (D_MODEL, 2 * N_GROUPS * B_WIDTH), D_MODEL ** -0.5),
        'k_norm': gain(ks[10], (N_GROUPS, B_HEAD_DIM)),
        'b_norm': gain(ks[11], (N_B_LAYERS, D_MODEL)),
        'b_w_q': nrm(ks[12], (N_B_LAYERS, D_MODEL, N_GROUPS * B_WIDTH), D_MODEL ** -0.5),
        'q_norm': gain(ks[13], (N_B_LAYERS, N_GROUPS, B_HEAD_DIM)),
        'b_w_out': nrm(ks[14], (N_B_LAYERS, B_WIDTH, D_MODEL), B_WIDTH ** -0.5 * out_scale),
        'f_norm': gain(ks[15], (DEPTH, D_MODEL)),
        'f_w_up': nrm(ks[16], (DEPTH, D_MODEL, 2 * D_FF), D_MODEL ** -0.5),
        'f_conv': nrm(ks[17], (DEPTH, CONV_WIDTH, 2 * D_FF), CONV_WIDTH ** -0.5),
        'f_conv_b': nrm(ks[18], (DEPTH, 2 * D_FF), 0.02),
        'f_w_down': nrm(ks[19], (DEPTH, D_FF, D_MODEL), D_FF ** -0.5 * out_scale),
    }


def reference(x, positions, a_norm, a_w_in, a_w_gate, a_b_gate, a_out_norm, a_w_out,
              kv_norm, w_kv, k_norm, b_norm, b_w_q, q_norm, b_w_out,
              f_norm, f_w_up, f_conv, f_conv_b, f_w_down):
    k_sh, v_sh = None, None
    for i in range(DEPTH):
        if i < N_A_LAYERS:
            x = x + gla_mixer(rms_norm(x, a_norm[i]), a_w_in[i], a_w_gate[i], a_b_gate[i],
                              a_out_norm[i], a_w_out[i])
        else:
            j = i - N_A_LAYERS
            if j == 0:
                k_sh, v_sh = shared_kv(x, kv_norm, w_kv, k_norm, positions)
            x = x + dilated_mixer(rms_norm(x, b_norm[j]), b_w_q[j], q_norm[j], b_w_out[j],
                                  k_sh, v_sh, positions)
        x = x + conv_ffn(rms_norm(x, f_norm[i]), f_w_up[i], f_conv[i], f_conv_b[i], f_w_down[i])
    return x
```

```python
import contextlib
import math
import numpy as np
import concourse.bass as bass
import concourse.mybir as mybir
from concourse.bass_utils import run_bass_kernel_spmd

F32 = mybir.dt.float32
BF16 = mybir.dt.bfloat16
I32 = mybir.dt.int32
AF = mybir.ActivationFunctionType
ALU = mybir.AluOpType

D = 1024
S = 2048
KC = 8
NSEQ = 2
DFF = 2816
NJ = 22
EPS = 1e-6
INW = 3088
TT = 512
NTT = S // TT
DIL = (1, 4, 16)
GCH = 128


class Dep:
    __slots__ = ("w", "r", "war")

    def __init__(self):
        self.w = {}
        self.r = {}
        self.war = {}


def _merge(dst, src):
    for k, v in src.items():
        if dst.get(k, 0) < v:
            dst[k] = v


class _Rec:
    def __init__(self):
        self.call = None

    def __getattr__(self, name):
        def f(*a, **k):
            assert self.call is None
            self.call = (name, a, k)
            return self
        return f


class KB:
    ENGS = ("pe", "act", "dve", "pool", "sp")

    def __init__(self, nc, stack):
        self.nc = nc
        self.stack = stack
        self.prog = {e: [] for e in self.ENGS}
        self.cnt = {}
        self.known = {e: {} for e in self.ENGS}
        self.semh = {}
        self.esem = {}
        for e in self.ENGS:
            if e != "sp":
                self.esem[e] = self.new_sem("s_" + e)
        self.pending_pe = False
        self.nops = 0

    def new_sem(self, name):
        h = self.stack.enter_context(self.nc.semaphore(name))
        self.semh[name] = h
        self.cnt[name] = 0
        return name

    def _emit_waits(self, e, waits):
        kn = self.known[e]
        for s, v in waits.items():
            if e == "pe" and self.esem.get(e) == s:
                continue
            if kn.get(s, 0) < v:
                kn[s] = v
                self.prog[e].append(("wait", s, v))

    def op(self, e, fn, reads=(), writes=(), partial=(), flag=True, dma=None):
        waits = {}
        for d in reads:
            _merge(waits, d.w)
        for d in writes:
            _merge(waits, d.w)
            _merge(waits, d.r)
            _merge(waits, d.war)
        for d in partial:
            _merge(waits, d.war)
            if d.r:
                _merge(waits, d.r)
                _merge(waits, d.w)
        self._emit_waits(e, waits)
        rec = _Rec()
        fn(rec)
        fn = rec.call
        if dma is not None:
            self.cnt[dma] += 16
            ev = (dma, self.cnt[dma])
            self.prog[e].append(("op", fn, dma, 16))
        else:
            s = self.esem[e]
            if flag:
                self.cnt[s] += 1
                ev = (s, self.cnt[s])
                self.prog[e].append(("op", fn, s, 1))
                if e == "pe":
                    self.pending_pe = False
            else:
                assert e == "pe"
                ev = (s, self.cnt[s] + 1)
                self.prog[e].append(("op", fn, None, 0))
                self.pending_pe = True
        evd = {ev[0]: ev[1]}
        for d in reads:
            _merge(d.r, evd)
        for d in writes:
            nw = {}
            _merge(nw, d.w)
            _merge(nw, d.r)
            d.war = nw
            d.w = dict(evd)
            d.r = {}
        for d in partial:
            if d.r:
                nw = {}
                _merge(nw, d.w)
                _merge(nw, d.r)
                _merge(nw, d.war)
                d.war = nw
                d.w = {}
                d.r = {}
            _merge(d.w, evd)
        self.nops += 1
        return ev

    def barrier(self):
        assert not self.pending_pe
        allv = {s: v for s, v in self.cnt.items() if v > 0}
        for e in self.ENGS:
            self._emit_waits(e, allv)

    def check(self):
        assert not self.pending_pe, "trailing unflagged PE instruction"
        val = {s: 0 for s in self.cnt}
        pc = {e: 0 for e in self.ENGS}
        progress = True
        while progress:
            progress = False
            for e in self.ENGS:
                p = self.prog[e]
                while pc[e] < len(p):
                    it = p[pc[e]]
                    if it[0] == "wait":
                        if val[it[1]] >= it[2]:
                            pc[e] += 1
                            progress = True
                        else:
                            break
                    else:
                        if it[2] is not None:
                            val[it[2]] += it[3]
                        pc[e] += 1
                        progress = True
        for e in self.ENGS:
            if pc[e] < len(self.prog[e]):
                raise RuntimeError(f"deadlock: {e} stuck at {pc[e]}/{len(self.prog[e])}: {self.prog[e][pc[e]][:3]}")

    def finish(self):
        self.barrier()
        self.check()
        nc = self.nc
        prog = self.prog
        semh = self.semh

        def replay(e):
            def body(engine):
                for it in prog[e]:
                    if it[0] == "wait":
                        engine.wait_ge(semh[it[1]], it[2])
                    else:
                        name, a, k = it[1]
                        ins = getattr(engine, name)(*a, **k)
                        if it[2] is not None:
                            ins.then_inc(semh[it[2]], it[3])
            return body

        with nc.Block() as block:
            block.tensor(replay("pe"))
            block.scalar(replay("act"))
            block.vector(replay("dve"))
            block.gpsimd(replay("pool"))
            block.sync(replay("sp"))


class Arena:
    def __init__(self, nc, stack, nbytes):
        self.words = nbytes // 4
        self.t = stack.enter_context(nc.sbuf_tensor("arena", [128, self.words], F32))
        self.off = 0
        self.peak = 0

    def alloc(self, shape, dt):
        n = int(np.prod(shape))
        esz = 2 if dt == BF16 else 4
        words = (n * esz + 3) // 4
        words = (words + 15) // 16 * 16
        if self.off + words > self.words:
            raise RuntimeError(f"arena overflow: need {self.off + words} words of {self.words}")
        ap = self.t[:, self.off:self.off + words]
        self.off += words
        self.peak = max(self.peak, self.off)
        if dt != F32:
            ap = ap.bitcast(dt)
        ap = ap[:, 0:n]
        if len(shape) == 2:
            ap = ap.rearrange("p (a b) -> p a b", a=shape[0])
        elif len(shape) == 3:
            ap = ap.rearrange("p (a b c) -> p a b c", a=shape[0], b=shape[1])
        return ap

    def mark(self):
        return self.off

    def release(self, m):
        self.off = m


class Ring:
    def __init__(self, items):
        self.items = items
        self.i = 0

    def next(self):
        it = self.items[self.i % len(self.items)]
        self.i += 1
        return it


SM = {}
_o = 0
for _n, _w in (("a_norm", 2 * 8), ("f_norm", 4 * 8), ("kv_norm", 8), ("b_norm", 2 * 8), ("k_norm", 3),
               ("q_norm", 6), ("o_norm", 4), ("conv", 4 * 44 * 4)):
    SM[_n] = _o
    _o += _w
NSMALL = _o

CS = {}
_o = 0
for _n, _w in (("ones", 128), ("tri", 128), ("ut", 128), ("bd", 128), ("perm", 128), ("invf", 1), ("sgn", 1),
               ("m2", 256), ("ident", 128), ("bd4", 512), ("eps", 1), ("zero", 1), ("lns", 1), ("one", 1)):
    CS[_n] = _o
    _o += _w
NCST = _o


def make_consts():
    c = np.zeros((128, NCST), np.float32)
    c[:, CS["ones"]:CS["ones"] + 128] = 1.0
    m = np.arange(128)[:, None]
    q = np.arange(128)[None, :]
    same = (m // GCH) == (q // GCH)
    c[:, CS["tri"]:CS["tri"] + 128] = np.where(same & (m <= q), -1.0 / 16.0, 0.0)
    c[:, CS["ut"]:CS["ut"] + 128] = np.where(same & (m > q), -1.0 / 16.0, 0.0)
    c[:, CS["bd"]:CS["bd"] + 128] = np.where(same & (m <= q), 1.0, 0.0)
    k32 = np.arange(32)[:, None]
    m32 = np.arange(32)[None, :]
    pm = np.eye(128, dtype=np.float32)
    pm[:32, :32] = (k32 == (m32 + 16) % 32).astype(np.float32)
    c[:, CS["perm"]:CS["perm"] + 128] = pm
    invf = 500000.0 ** (-(np.arange(16, dtype=np.float64) * 2.0 / 32.0))
    c[:32, CS["invf"]] = np.concatenate([invf, invf]).astype(np.float32)
    c[:32, CS["sgn"]] = np.concatenate([-np.ones(16), np.ones(16)]).astype(np.float32)
    c[:, CS["m2"]:CS["m2"] + 128] = np.where(m <= q, 0.0, -30000.0)
    c[:, CS["m2"] + 128:CS["m2"] + 256] = np.where(m >= q, 0.0, -30000.0)
    c[:, CS["ident"]:CS["ident"] + 128] = np.eye(128, dtype=np.float32)
    c[:, CS["bd4"]:CS["bd4"] + 512] = np.tile(np.where(same & (m <= q), 1.0, 0.0), (1, 4))
    c[:, CS["eps"]] = EPS
    c[:, CS["zero"]] = 0.0
    c[:, CS["lns"]] = math.log(128.0 ** -0.5)
    c[:, CS["one"]] = 1.0
    return c


IN_SHAPES = {"posb": ((NSEQ, 32, S), I32), "wk": ((24, 128, KC, 128), F32), "wv": ((3, 128, KC, 1024), F32)}
for _l in range(2):
    IN_SHAPES[f"wg{_l}"] = ((33, 512), F32)
    IN_SHAPES[f"win{_l}"] = ((128, KC, INW), F32)
    IN_SHAPES[f"awo{_l}"] = ((128, KC, D), F32)
    IN_SHAPES[f"wq{_l}"] = ((24, 128, KC, 128), F32)
    IN_SHAPES[f"bwo{_l}"] = ((128, KC, D), F32)
for _l in range(4):
    IN_SHAPES[f"wup{_l}"] = ((NJ, 128, 2 * KC * 128), F32)
    IN_SHAPES[f"wdn{_l}"] = ((128, NJ, D), F32)


class Prog:
    def __init__(self, phases, nseq=NSEQ, arena_kib=206):
        self.phases = phases
        self.nseq = nseq
        nc = bass.Bass("TRN2", target_bir_lowering=False)
        self.nc = nc
        dt = nc.dram_tensor
        self.xT = dt("xT", [nseq, 128, KC, S], F32, kind="ExternalInput").ap()
        self.yT = dt("yT", [nseq, 128, KC, S], F32, kind="ExternalOutput").ap()
        self.small = dt("small", [128, NSMALL], F32, kind="ExternalInput").ap()
        self.cst = dt("cst", [128, NCST], F32, kind="ExternalInput").ap()
        self.inputs = {}
        self._scratch = {}

        with contextlib.ExitStack() as st:
            kb = KB(nc, st)
            self.kb = kb
            self.ar = Arena(nc, st, arena_kib * 1024)
            self.pbank = [st.enter_context(nc.psum_tensor(f"pb{i}", [128, 512], F32)) for i in range(8)]
            self.pdep = [Dep() for _ in range(8)]
            self.ring_mm = Ring([0, 1, 2, 3])
            self.ring_aux = Ring([4, 5, 6, 7])
            self.ring_all = Ring([0, 1, 2, 3, 4, 5, 6, 7])
            self.sem_pool, self.phase_sems, self.nsem = [], [], 0
            self.sem_c = kb.new_sem("d_const")
            self.sem_x = [kb.new_sem(f"d_x{c}") for c in range(KC)]
            self.sem_out = kb.new_sem("d_out")
            self.build()
            kb.finish()

    def din(self, name):
        if name not in self.inputs:
            shp = list(IN_SHAPES[name][0])
            if name == "posb":
                shp[0] = self.nseq
            self.inputs[name] = self.nc.dram_tensor(name, shp, IN_SHAPES[name][1], kind="ExternalInput").ap()
        return self.inputs[name]

    def scratch(self, name, shape, dt):
        if name not in self._scratch:
            self._scratch[name] = self.nc.dram_tensor(name, list(shape), dt, kind="Internal").ap()
        return self._scratch[name]

    def ps(self, ring="mm"):
        i = {"mm": self.ring_mm, "aux": self.ring_aux, "all": self.ring_all}[ring].next()
        return self.pbank[i], self.pdep[i]

    def dma_sem(self, name):
        if self.sem_pool:
            sname = self.sem_pool.pop()
        else:
            sname = self.kb.new_sem(f"dq{self.nsem}")
            self.nsem += 1
        self.phase_sems.append(sname)
        return sname

    def build(self):
        kb, ar = self.kb, self.ar
        self.c32 = ar.alloc((NCST,), F32)
        self.sm = ar.alloc((NSMALL,), F32)
        self.cb = ar.alloc((NCST,), BF16)
        self.x = ar.alloc((KC, S), F32)
        self.dconst = Dep()
        self.dx = [[Dep() for _ in range(NTT)] for _ in range(KC)]
        kb.op("sp", lambda e: e.dma_start(out=self.c32, in_=self.cst[:, :]), partial=[self.dconst], dma=self.sem_c)
        kb.op("sp", lambda e: e.dma_start(out=self.sm, in_=self.small[:, :]), partial=[self.dconst], dma=self.sem_c)
        kb.op("dve", lambda e: e.tensor_copy(out=self.cb, in_=self.c32), reads=[self.dconst], partial=[self.dconst])
        self.ones_bf = self.cb[:, CS["ones"]:CS["ones"] + 128]
        base = ar.mark()
        for b in range(self.nseq):
            for c in range(KC):
                kb.op("sp", lambda e, b=b, c=c: e.dma_start(out=self.x[:, c, :], in_=self.xT[b, :, c, :]),
                      writes=self.dx[c], dma=self.sem_x[c])
            self.rope_ready = False
            for pidx, ph in enumerate(self.phases):
                kind = ph[0]
                m = ar.mark()
                if kind == "ffn":
                    nxt = self.phases[pidx + 1][0] if pidx + 1 < len(self.phases) else None
                    self.ffn(ph[1], rope_b=(b if nxt == "kv" else None))
                elif kind == "gla":
                    self.gla(ph[1])
                elif kind == "kv":
                    self.kv(b)
                elif kind == "dil":
                    self.dil(ph[1], b)
                kb.barrier()
                ar.release(m)
                self.sem_pool.extend(self.phase_sems)
                self.phase_sems = []
            for c in range(KC):
                kb.op("sp", lambda e, b=b, c=c: e.dma_start(out=self.yT[b, :, c, :], in_=self.x[:, c, :]),
                      reads=self.dx[c], dma=self.sem_out)
            kb.barrier()
            ar.release(base)

    def norm_stats(self, srcs, src_deps, n, dim, bufs, extra_scale=1.0):
        kb = self.kb
        pss, dps = self.ps("aux")
        nsrc = len(srcs)
        for i, (s_ap, s_dep) in enumerate(zip(srcs, src_deps)):
            sq, dsq = bufs["sq"].next()
            kb.op("act", lambda e, sq=sq, s_ap=s_ap: e.activation(out=sq[:, 0:n], in_=s_ap, func=AF.Square),
                  reads=s_dep, writes=[dsq])
            kb.op("pe", lambda e, sq=sq, i=i: e.matmul(pss[:, 0:n], lhsT=self.ones_bf, rhs=sq[:, 0:n],
                                                       start=(i == 0), stop=(i == nsrc - 1)),
                  reads=[dsq, self.dconst], partial=[dps], flag=True)
        lnv, dln = bufs["lnv"].next()
        rstd, drs = bufs["rstd"].next()
        kb.op("act", lambda e: e.activation(out=lnv[:, 0:n], in_=pss[:, 0:n], func=AF.Ln, scale=1.0 / dim, bias=self.epsb),
              reads=[dps, self.dconst], writes=[dln])
        kb.op("act", lambda e: e.activation(out=rstd[:, 0:n], in_=lnv[:, 0:n], func=AF.Exp, scale=-0.5,
                                            bias=(self.lnsb if extra_scale != 1.0 else self.zerob)),
              reads=[dln, self.dconst], writes=[drs])
        return rstd, drs

    def norm_bufs(self, n=TT, slim=False):
        ar = self.ar
        return {
            "sq": Ring([(ar.alloc((n,), BF16), Dep()) for _ in range(2 if slim else 3)]),
            "lnv": Ring([(ar.alloc((n,), F32), Dep()) for _ in range(1 if slim else 2)]),
            "rstd": Ring([(ar.alloc((n,), F32), Dep()) for _ in range(2 if slim else 3)]),
        }

    def small_consts(self):
        c = self.c32
        self.epsb = c[:, CS["eps"]:CS["eps"] + 1]
        self.zerob = c[:, CS["zero"]:CS["zero"] + 1]
        self.lnsb = c[:, CS["lns"]:CS["lns"] + 1]
        self.oneb = c[:, CS["one"]:CS["one"] + 1]

    def lazy_norm(self, gain_off, xn, dxn, nb):
        done = set()

        def need(tt):
            if tt not in done:
                done.add(tt)
                self.norm_x(gain_off, xn, dxn, nb, tts=[tt])
        return need

    def norm_x(self, gain_off, xn, dxn, nb, tts=range(NTT)):
        kb = self.kb
        for tt in tts:
            cs = slice(tt * TT, (tt + 1) * TT)
            rstd, drs = self.norm_stats([self.x[:, c, cs] for c in range(KC)], [[self.dx[c][tt]] for c in range(KC)],
                                        TT, D, nb)
            for c in range(KC):
                g = self.sm[:, gain_off + c:gain_off + c + 1]
                kb.op("dve", lambda e, c=c, g=g, rstd=rstd, cs=cs: e.scalar_tensor_tensor(
                    out=xn[:, c, cs], in0=self.x[:, c, cs], scalar=g, in1=rstd[:, 0:TT], op0=ALU.mult, op1=ALU.mult),
                    reads=[self.dx[c][tt], drs, self.dconst], writes=[dxn[c][tt]])

    def ffn(self, L, rope_b=None):
        kb, ar = self.kb, self.ar
        self.small_consts()
        rope_todo = self.rope_chunks(rope_b) if rope_b is not None else iter(())
        nb = self.norm_bufs(slim=True)
        xn = ar.alloc((KC, S), BF16)
        dxn = [[Dep() for _ in range(NTT)] for _ in range(KC)]
        parts = [(0, 6), (6, 12), (12, 17), (17, 22)]
        PJ = 6
        a = ar.alloc((PJ, S), BF16)
        da = [[Dep() for _ in range(NTT)] for _ in range(PJ)]
        NWS = 3
        wslots = [(ar.alloc((2 * KC * 128,), BF16), Dep(), self.dma_sem(f"f{L}_wu{i}_{kb.nops}")) for i in range(NWS)]
        wd = [(ar.alloc((PJ, D), BF16), Dep(), self.dma_sem(f"f{L}_wd{i}_{kb.nops}")) for i in range(2)]
        hals = [(ar.alloc((2 * NTT,), F32), [Dep() for _ in range(NTT)]) for _ in range(2)]
        tg = Ring([(ar.alloc((TT,), F32), Dep()) for _ in range(3)])
        tv = Ring([(ar.alloc((TT,), F32), Dep()) for _ in range(3)])
        sg = Ring([(ar.alloc((TT,), F32), Dep()) for _ in range(2)])

        def load_wu(j):
            w, dw, sem = wslots[j % NWS]
            kb.op("pool", lambda e, w=w, j=j: e.dma_start(out=w, in_=self.din(f'wup{L}')[j, :, :], max_dma_last_dim=4096),
                  writes=[dw], dma=sem)

        def load_wd(pi):
            j0, j1 = parts[pi]
            w, dw, sem = wd[pi % 2]
            kb.op("pool", lambda e, w=w: e.dma_start(out=w[:, 0:j1 - j0, :], in_=self.din(f'wdn{L}')[:, j0:j1, :],
                                                      max_dma_last_dim=4096), writes=[dw], dma=sem)

        load_wu(0)
        load_wu(1)
        load_wd(0)
        need_norm = self.lazy_norm(SM["f_norm"] + L * 8, xn, dxn, nb)
        cpo = SM["conv"] + L * 44 * 4

        def cparam(chunk, k):
            o = cpo + chunk * 4 + k
            return self.sm[:, o:o + 1]

        steps = [(pi, j, tt) for pi, (j0, j1) in enumerate(parts) for j in range(j0, j1) for tt in range(NTT)]
        last_of_part = {}
        for idx, (pi, j, tt) in enumerate(steps):
            last_of_part[pi] = idx
        LA = 2
        down_at = {}
        for pi in range(len(parts)):
            down_at.setdefault(min(last_of_part[pi] + LA, len(steps) - 1), []).append(pi)
        pending = []
        down_done = [False] * len(parts)

        def flush(keep, maxpi=None):
            while len(pending) > keep:
                p_, fn_ = pending[0]
                if maxpi is not None and p_ > maxpi:
                    break
                if p_ > 0 and not down_done[p_ - 1]:
                    break
                pending.pop(0)
                fn_()

        def down(pi):
            j0, j1 = parts[pi]
            wdt, dwd, _ = wd[pi % 2]
            nj = j1 - j0
            for tt in range(NTT):
                cs = slice(tt * TT, (tt + 1) * TT)
                for c in range(KC):
                    pt, dp = self.ps("all")
                    for jj in range(nj):
                        kb.op("pe", lambda e: e.matmul(pt[:, :], lhsT=wdt[:, jj, c * 128:(c + 1) * 128], rhs=a[:, jj, cs],
                                                       start=(jj == 0), stop=(jj == nj - 1)),
                              reads=[dwd, da[jj][tt]], partial=[dp], flag=(jj == nj - 1))
                    kb.op("dve", lambda e: e.tensor_tensor(out=self.x[:, c, cs], in0=pt[:, :], in1=self.x[:, c, cs], op=ALU.add),
                          reads=[dp], writes=[self.dx[c][tt]])
            down_done[pi] = True
            if pi + 2 < len(parts):
                load_wd(pi + 2)

        load_wd(1)
        for idx, (pi, j, tt) in enumerate(steps):
            j0, j1 = parts[pi]
            jj = j - j0
            if tt == 0 and j + 2 < NJ:
                load_wu(j + 2)
            w, dw, _ = wslots[j % NWS]
            need_norm(tt)
            cs = slice(tt * TT, (tt + 1) * TT)
            outs = []
            for half in range(2):
                pt, dp = self.ps("all")
                for kc in range(KC):
                    kb.op("pe", lambda e: e.matmul(pt[:, :], lhsT=w[:, half * 1024 + kc * 128: half * 1024 + (kc + 1) * 128],
                                                   rhs=xn[:, kc, cs], start=(kc == 0), stop=(kc == KC - 1)),
                          reads=[dw, dxn[kc][tt]], partial=[dp], flag=(kc == KC - 1))
                chunk = j + half * NJ
                t1, dt1 = (tg if half == 0 else tv).next()
                hal, dhal = hals[half]
                kb.op("act", lambda e: e.activation(out=t1, in_=pt[:, :], func=AF.Identity, scale=cparam(chunk, 2), bias=cparam(chunk, 3)),
                      reads=[dp, self.dconst], writes=[dt1])
                if tt < NTT - 1:
                    kb.op("act", lambda e: e.activation(out=hal[:, 2 * tt:2 * tt + 2], in_=pt[:, TT - 2:TT], func=AF.Identity),
                          reads=[dp], writes=[dhal[tt]])
                outs.append((t1, dt1, pt, dp, hal, dhal, chunk))
            for k in (1, 0):
                sh = 2 - k
                for (t1, dt1, pt, dp, hal, dhal, chunk) in outs:
                    kb.op("dve", lambda e: e.scalar_tensor_tensor(out=t1[:, sh:TT], in0=pt[:, 0:TT - sh], scalar=cparam(chunk, k), in1=t1[:, sh:TT],
                                                                  op0=ALU.mult, op1=ALU.add),
                          reads=[dp, self.dconst], writes=[dt1])
            if tt > 0:
                for (t1, dt1, pt, dp, hal, dhal, chunk) in outs:
                    hp = hal[:, 2 * (tt - 1):2 * (tt - 1) + 2]
                    kb.op("dve", lambda e: e.scalar_tensor_tensor(out=t1[:, 0:1], in0=hp[:, 1:2], scalar=cparam(chunk, 1), in1=t1[:, 0:1],
                                                                  op0=ALU.mult, op1=ALU.add),
                          reads=[dhal[tt - 1], self.dconst], writes=[dt1])
                    kb.op("dve", lambda e: e.scalar_tensor_tensor(out=t1[:, 0:2], in0=hp[:, 0:2], scalar=cparam(chunk, 0), in1=t1[:, 0:2],
                                                                  op0=ALU.mult, op1=ALU.add),
                          reads=[dhal[tt - 1], self.dconst], writes=[dt1])
            (t1g, dt1g), (t1v, dt1v) = [(o_[0], o_[1]) for o_ in outs]

            def tail(t1g=t1g, dt1g=dt1g, t1v=t1v, dt1v=dt1v, jj=jj, cs=cs, tt=tt):
                sgt, dsg = sg.next()
                kb.op("act", lambda e: e.activation(out=sgt, in_=t1g, func=AF.Silu), reads=[dt1g], writes=[dsg])
                kb.op("pool", lambda e: e.tensor_tensor(out=a[:, jj, cs], in0=t1v, in1=sgt, op=ALU.mult),
                      reads=[dsg, dt1v], writes=[da[jj][tt]])
            pending.append((pi, tail))
            flush(keep=1)
            next(rope_todo, None)
            if idx % 4 == 1:
                next(rope_todo, None)
            for p_ in down_at.get(idx, []):
                flush(keep=0, maxpi=p_)
                down(p_)
                flush(keep=1)
        assert not pending and all(down_done)
        for _ in rope_todo:
            pass
        if rope_b is not None:
            self.rope_ready = True

    def gla(self, L):
        kb, ar = self.kb, self.ar
        self.small_consts()
        c32 = self.c32
        TG = 256
        NTG = S // TG
        nb = self.norm_bufs(TG)
        win_h = self.din(f"win{L}")
        awo_h = self.din(f"awo{L}")
        wg_h = self.din(f"wg{L}")
        win = ar.alloc((KC, INW), BF16)
        wout = ar.alloc((KC, D), BF16)
        wgs = ar.alloc((512,), F32)
        dwin, dwout, dwg = Dep(), Dep(), Dep()
        s_win, s_wout, s_wg = (self.dma_sem(f"g{L}_{n}_{kb.nops}") for n in ("win", "wout", "wg"))
        for kc in range(KC):
            kb.op("pool", lambda e, kc=kc: e.dma_start(out=win[:, kc, :], in_=win_h[:, kc, :], max_dma_last_dim=4096),
                  partial=[dwin], dma=s_win)
        kb.op("sp", lambda e: e.dma_start(out=wgs[0:33, :], in_=wg_h[:, :]), writes=[dwg], dma=s_wg)
        dwin_g = {"r": dwin, "qk": dwin, "v": dwin}
        for kc in range(KC):
            kb.op("pool", lambda e, kc=kc: e.dma_start(out=wout[:, kc, :], in_=awo_h[:, kc, :], max_dma_last_dim=4096),
                  partial=[dwout], dma=s_wout)
        xn = Ring([(ar.alloc((KC, TG), BF16), Dep())])
        glr = ar.alloc((TG,), F32)
        dglr = Dep()
        kb.op("pool", lambda e: e.memset(glr[0:64, :], 0.0), writes=[dglr])
        kb.op("pool", lambda e: e.memset(glr[32:33, :], 1.0), partial=[dglr])
        e1 = (ar.alloc((512,), F32), Dep())
        lsb = (ar.alloc((512,), F32), Dep())
        Eq = (ar.alloc((4, TG), F32), Dep())
        Ek = (ar.alloc((4, TG), F32), Dep())
        Es = (ar.alloc((2, 512), F32), Dep())
        qt = (ar.alloc((4, TG), BF16), Dep())
        kt = (ar.alloc((4, TG), BF16), Dep())
        ks = Ring([(ar.alloc((512,), BF16), Dep()) for _ in range(2)])
        vb = Ring([(ar.alloc((1024,), BF16), Dep()) for _ in range(2)])
        sr = (ar.alloc((KC, TG), BF16), Dep())
        og = (ar.alloc((KC, TG), BF16), Dep())
        attm = Ring([(ar.alloc((4, 128), BF16), Dep()) for _ in range(2)])
        S32 = ar.alloc((4, 256), F32)
        dS32 = [Dep() for _ in range(4)]
        Sb = Ring([(ar.alloc((4, 256), BF16), [Dep() for _ in range(4)]) for _ in range(2)])
        sqo = (ar.alloc((1024,), BF16), Dep())
        lnvo = (ar.alloc((512,), F32), Dep())
        rso = (ar.alloc((4, 128), F32), Dep())
        tmpo = Ring([(ar.alloc((2, 128), F32), Dep()) for _ in range(2)])
        go = SM["o_norm"] + L * 2
        QS = 128.0 ** -0.5
        tiles = [dict() for _ in range(NTG)]
        st = dict(first_chunk=True, cur_Sb=None)

        def stage_A(t):
            T = tiles[t]
            cs = slice(t * TG, (t + 1) * TG)
            tt = (t * TG) // TT
            rstd, drs = self.norm_stats([self.x[:, c, cs] for c in range(KC)], [[self.dx[c][tt]] for c in range(KC)], TG, D, nb)
            xnt, dxn = xn.next()
            T["xn"] = (xnt, dxn)
            for c in range(KC):
                g = self.sm[:, SM["a_norm"] + L * 8 + c:SM["a_norm"] + L * 8 + c + 1]
                kb.op("dve", lambda e: e.scalar_tensor_tensor(out=xnt[:, c, :], in0=self.x[:, c, cs], scalar=g, in1=rstd[:, 0:TG],
                                                              op0=ALU.mult, op1=ALU.mult),
                      reads=[self.dx[c][tt], drs, self.dconst], partial=[dxn])
            pg, dpg = self.ps("aux")
            for kc in range(KC):
                kb.op("pe", lambda e: e.matmul(pg[0:16, 0:TG], lhsT=win[:, kc, 2048:2064], rhs=xnt[:, kc, :], start=(kc == 0), stop=(kc == KC - 1)),
                      reads=[dwin_g["r"], dxn], partial=[dpg], flag=(kc == KC - 1))
            kb.op("act", lambda e: e.activation(out=glr[0:16, :], in_=pg[0:16, 0:TG], func=AF.Identity), reads=[dpg], partial=[dglr])
            for f in range(8):
                pr, dpr = self.ps("mm")
                for kc in range(KC):
                    kb.op("pe", lambda e: e.matmul(pr[:, 0:TG], lhsT=win[:, kc, 2064 + f * 128:2064 + (f + 1) * 128], rhs=xnt[:, kc, :],
                                                   start=(kc == 0), stop=(kc == KC - 1)),
                          reads=[dwin_g["r"], dxn], partial=[dpr], flag=(kc == KC - 1))
                kb.op("act", lambda e: e.activation(out=sr[0][:, f, :], in_=pr[:, 0:TG], func=AF.Silu), reads=[dpr], partial=[sr[1]])

        def stage_B(t):
            T = tiles[t]
            xnt, dxn = T["xn"]
            T["vb"], T["ks"] = {}, {}

            def LG(bk):
                bs = slice(bk * 128, (bk + 1) * 128)
                pl, dpl = self.ps("aux")
                kb.op("pe", lambda e: e.matmul(pl[:, :], lhsT=glr[0:33, bs], rhs=wgs[0:33, :], start=True, stop=True),
                      reads=[dglr, dwg], writes=[dpl])
                kb.op("act", lambda e: e.activation(out=e1[0], in_=pl[:, :], func=AF.Exp, scale=-1.0), reads=[dpl], writes=[e1[1]])
                kb.op("act", lambda e: e.activation(out=lsb[0], in_=e1[0], func=AF.Ln, bias=self.oneb),
                      reads=[e1[1], self.dconst], writes=[lsb[1]])

            def CS_(bk):
                bs = slice(bk * 128, (bk + 1) * 128)
                pb, dpb = self.ps("aux")
                for h in range(4):
                    kb.op("pe", lambda e: e.matmul(pb[:, h * 128:(h + 1) * 128], lhsT=lsb[0][:, h * 128:(h + 1) * 128],
                                                   rhs=c32[:, CS["tri"]:CS["tri"] + 128], start=True, stop=True),
                          reads=[lsb[1], self.dconst], partial=[dpb], flag=(h == 3))
                pbv = pb[:, :].rearrange("p (h c) -> p h c", h=4)
                kb.op("act", lambda e: e.activation(out=Eq[0][:, :, bs], in_=pbv, func=AF.Exp), reads=[dpb], partial=[Eq[1]])
                kb.op("act", lambda e: e.activation(out=Ek[0][:, :, bs], in_=pbv, func=AF.Exp, scale=-1.0), reads=[dpb], partial=[Ek[1]])
                pu, dpu = self.ps("aux")
                kb.op("pe", lambda e: e.matmul(pu[:, :], lhsT=c32[:, CS["ut"]:CS["ut"] + 128], rhs=lsb[0], start=True, stop=True),
                      reads=[lsb[1], self.dconst], writes=[dpu])
                kb.op("act", lambda e: e.activation(out=Es[0][:, bk, :], in_=pu[:, :], func=AF.Exp), reads=[dpu], partial=[Es[1]])

            def VT(bk):
                bs = slice(bk * 128, (bk + 1) * 128)
                vbt, dvb = vb.next()
                T["vb"][bk] = (vbt, dvb)
                for i in range(2):
                    pv, dpv = self.ps("mm")
                    for kc in range(KC):
                        kb.op("pe", lambda e: e.matmul(pv[:, :], lhsT=xnt[:, kc, bs], rhs=win[:, kc, 1024 + i * 512:1024 + (i + 1) * 512],
                                                       start=(kc == 0), stop=(kc == KC - 1)),
                              reads=[dwin_g["v"], dxn], partial=[dpv], flag=(kc == KC - 1))
                    kb.op("act", lambda e: e.activation(out=vbt[:, i * 512:(i + 1) * 512], in_=pv[:, :], func=AF.Identity),
                          reads=[dpv], partial=[dvb])

            def KT(bk):
                bs = slice(bk * 128, (bk + 1) * 128)
                pk, dpk = self.ps("mm")
                for kc in range(KC):
                    kb.op("pe", lambda e: e.matmul(pk[:, :], lhsT=xnt[:, kc, bs], rhs=win[:, kc, 512:1024], start=(kc == 0), stop=(kc == KC - 1)),
                          reads=[dwin_g["qk"], dxn], partial=[dpk], flag=(kc == KC - 1))
                kst, dks = ks.next()
                T["ks"][bk] = (kst, dks)
                kb.op("dve", lambda e: e.tensor_tensor(out=kst, in0=pk[:, :], in1=Es[0][:, bk, :], op=ALU.mult),
                      reads=[dpk, Es[1]], writes=[dks])

            LG(0)
            VT(0)
            CS_(0)
            LG(1)
            VT(1)
            CS_(1)
            KT(0)
            KT(1)
            for oc in range(8):
                pq, dpq = self.ps("mm")
                for kc in range(KC):
                    kb.op("pe", lambda e: e.matmul(pq[:, 0:TG], lhsT=win[:, kc, oc * 128:(oc + 1) * 128], rhs=xnt[:, kc, :],
                                                   start=(kc == 0), stop=(kc == KC - 1)),
                          reads=[dwin_g["qk"], dxn], partial=[dpq], flag=(kc == KC - 1))
                if oc < 4:
                    kb.op("dve", lambda e: e.scalar_tensor_tensor(out=qt[0][:, oc, :], in0=pq[:, 0:TG], scalar=QS, in1=Eq[0][:, oc, :],
                                                                  op0=ALU.mult, op1=ALU.mult), reads=[dpq, Eq[1]], partial=[qt[1]])
                else:
                    kb.op("dve", lambda e: e.tensor_tensor(out=kt[0][:, oc - 4, :], in0=pq[:, 0:TG], in1=Ek[0][:, oc - 4, :], op=ALU.mult),
                          reads=[dpq, Ek[1]], partial=[kt[1]])

        def stage_C(t):
            T = tiles[t]
            for bk in range(2):
                bs = slice(bk * 128, (bk + 1) * 128)
                kst, dks = T["ks"][bk]
                vbt, dvb = T["vb"][bk]
                first_chunk = st["first_chunk"]
                pa, dpa = self.ps("aux")
                for h in range(4):
                    kb.op("pe", lambda e: e.matmul(pa[:, h * 128:(h + 1) * 128], lhsT=kt[0][:, h, bs], rhs=qt[0][:, h, bs], start=True, stop=True),
                          reads=[kt[1], qt[1]], partial=[dpa], flag=(h == 3))
                am, dam = attm.next()
                kb.op("dve", lambda e: e.tensor_tensor(
                    out=am, in0=pa[:, :].rearrange("p (h c) -> p h c", h=4),
                    in1=c32[:, CS["bd4"]:CS["bd4"] + 512].rearrange("p (h c) -> p h c", h=4), op=ALU.mult),
                    reads=[dpa, self.dconst], writes=[dam])
                qcols = slice(bk * 128, (bk + 1) * 128)
                po2 = [self.ps("aux"), self.ps("aux")]
                for h in range(4):
                    po, dpo = po2[h // 2]
                    for vc in range(2):
                        oc_ = slice(((h % 2) * 2 + vc) * 128, ((h % 2) * 2 + vc + 1) * 128)
                        last = (h % 2 == 1 and vc == 1)
                        kb.op("pe", lambda e: e.matmul(po[:, oc_], lhsT=vbt[:, h * 256 + vc * 128:h * 256 + (vc + 1) * 128], rhs=am[:, h, :],
                                                       start=True, stop=first_chunk), reads=[dvb, dam], partial=[dpo], flag=(last and first_chunk))
                        if not first_chunk:
                            sbt, dsb = st["cur_Sb"]
                            kb.op("pe", lambda e: e.matmul(po[:, oc_], lhsT=sbt[:, h, vc * 128:(vc + 1) * 128], rhs=qt[0][:, h, qcols],
                                                           start=False, stop=True), reads=[dsb[h], qt[1]], partial=[dpo], flag=last)
                nsb, dnsb = Sb.next()
                for h in range(4):
                    pst, dpst = self.ps("mm")
                    kb.op("pe", lambda e: e.matmul(pst[:, 0:256], lhsT=kst[:, h * 128:(h + 1) * 128], rhs=vbt[:, h * 256:(h + 1) * 256],
                                                   start=True, stop=True), reads=[dks, dvb], writes=[dpst])
                    if first_chunk:
                        kb.op("dve", lambda e: e.tensor_copy(out=S32[:, h, :], in_=pst[:, 0:256]), reads=[dpst], writes=[dS32[h]])
                    else:
                        dcol = bk * 128 + 127
                        kb.op("dve", lambda e: e.scalar_tensor_tensor(
                            out=S32[:, h, :], in0=S32[:, h, :], scalar=Eq[0][:, h, dcol:dcol + 1], in1=pst[:, 0:256],
                            op0=ALU.mult, op1=ALU.add), reads=[dpst, Eq[1]], writes=[dS32[h]])
                    kb.op("pool", lambda e: e.tensor_copy(out=nsb[:, h, :], in_=S32[:, h, :]), reads=[dS32[h]], writes=[dnsb[h]])
                st["cur_Sb"] = (nsb, dnsb)
                st["first_chunk"] = False
                pss, dpss = self.ps("aux")
                for i in range(2):
                    po, dpo = po2[i]
                    kb.op("act", lambda e: e.activation(out=sqo[0][:, i * 512:(i + 1) * 512], in_=po[:, :], func=AF.Square),
                          reads=[dpo], partial=[sqo[1]])
                for h in range(4):
                    for vc in range(2):
                        kb.op("pe", lambda e: e.matmul(pss[:, h * 128:(h + 1) * 128], lhsT=self.ones_bf,
                                                       rhs=sqo[0][:, (h * 2 + vc) * 128:(h * 2 + vc + 1) * 128],
                                                       start=(vc == 0), stop=(vc == 1)),
                              reads=[sqo[1], self.dconst], partial=[dpss], flag=(h == 3 and vc == 1))
                kb.op("act", lambda e: e.activation(out=lnvo[0], in_=pss[:, :], func=AF.Ln, scale=1.0 / 256.0, bias=self.epsb),
                      reads=[dpss, self.dconst], writes=[lnvo[1]])
                kb.op("act", lambda e: e.activation(out=rso[0].rearrange("p h c -> p (h c)"), in_=lnvo[0], func=AF.Exp, scale=-0.5),
                      reads=[lnvo[1]], writes=[rso[1]])
                ogv = og[0].rearrange("p (h v) c -> p h v c", v=2)
                srv = sr[0].rearrange("p (h v) c -> p h v c", v=2)
                for i in range(2):
                    po, dpo = po2[i]
                    pov = po[:, :].rearrange("p (h v c) -> p h v c", h=2, v=2)
                    for vc in range(2):
                        tm, dtm = tmpo.next()
                        g = self.sm[:, go + vc:go + vc + 1]
                        kb.op("dve", lambda e: e.scalar_tensor_tensor(
                            out=tm, in0=pov[:, :, vc, :], scalar=g, in1=rso[0][:, 2 * i:2 * i + 2, :], op0=ALU.mult, op1=ALU.mult),
                            reads=[dpo, rso[1], self.dconst], writes=[dtm])
                        kb.op("pool", lambda e: e.tensor_tensor(
                            out=ogv[:, 2 * i:2 * i + 2, vc, qcols], in0=tm, in1=srv[:, 2 * i:2 * i + 2, vc, qcols], op=ALU.mult),
                            reads=[dtm, sr[1]], partial=[og[1]])

        def stage_W(t):
            cs = slice(t * TG, (t + 1) * TG)
            tt = (t * TG) // TT
            for c in range(KC):
                pw, dpw = self.ps("mm")
                for f in range(KC):
                    kb.op("pe", lambda e: e.matmul(pw[:, 0:TG], lhsT=wout[:, f, c * 128:(c + 1) * 128], rhs=og[0][:, f, :],
                                                   start=(f == 0), stop=(f == KC - 1)),
                          reads=[dwout, og[1]], partial=[dpw], flag=(f == KC - 1))
                kb.op("dve", lambda e: e.tensor_tensor(out=self.x[:, c, cs], in0=pw[:, 0:TG], in1=self.x[:, c, cs], op=ALU.add),
                      reads=[dpw], writes=[self.dx[c][tt]])

        stage_A(0)
        for t in range(NTG):
            stage_B(t)
            stage_C(t)
            if t + 1 < NTG:
                stage_A(t + 1)
            stage_W(t)


    def rope_compute(self, b):
        kb, ar = self.kb, self.ar
        c32 = self.c32
        m = ar.mark()
        cos2 = ar.alloc((S,), F32)
        sin2 = ar.alloc((S,), F32)
        drope = Dep()
        posi = ar.alloc((S,), I32) if False else ar.alloc((S,), F32).bitcast(I32)
        y = ar.alloc((S,), F32)
        r_ = ar.alloc((S,), F32)
        t_ = ar.alloc((S,), F32)
        dp_, dy, dr, dt_ = Dep(), Dep(), Dep(), Dep()
        sem = self.dma_sem(f"pos_{kb.nops}")
        R = slice(0, 32)
        invf = c32[R, CS["invf"]:CS["invf"] + 1]
        sgn = c32[R, CS["sgn"]:CS["sgn"] + 1]
        kb.op("sp", lambda e: e.dma_start(out=posi[R, :], in_=self.din("posb")[b, :, :]), writes=[dp_], dma=sem)
        kb.op("dve", lambda e: e.tensor_copy(out=t_[R, :], in_=posi[R, :]), reads=[dp_], writes=[dt_])
        kb.op("dve", lambda e: e.tensor_scalar(out=y[R, :], in0=t_[R, :], scalar1=invf, scalar2=1.0 / (2.0 * math.pi),
                                               op0=ALU.mult, op1=ALU.mult), reads=[dt_, self.dconst], writes=[dy])
        for which, dst in (("sin", sin2), ("cos", cos2)):
            if which == "cos":
                kb.op("dve", lambda e: e.tensor_scalar(out=y[R, :], in0=y[R, :], scalar1=0.25, scalar2=None, op0=ALU.add),
                      reads=[dy], writes=[dy])
            yi = r_.bitcast(I32)
            kb.op("dve", lambda e: e.tensor_copy(out=yi[R, :], in_=y[R, :]), reads=[dy], writes=[dr])
            kb.op("dve", lambda e: e.tensor_copy(out=t_[R, :], in_=yi[R, :]), reads=[dr], writes=[dt_])
            kb.op("dve", lambda e: e.tensor_tensor(out=r_[R, :], in0=y[R, :], in1=t_[R, :], op=ALU.subtract),
                  reads=[dy, dt_], writes=[dr])
            kb.op("dve", lambda e: e.tensor_scalar(out=t_[R, :], in0=r_[R, :], scalar1=0.5, scalar2=None, op0=ALU.is_gt),
                  reads=[dr], writes=[dt_])
            kb.op("dve", lambda e: e.tensor_tensor(out=r_[R, :], in0=r_[R, :], in1=t_[R, :], op=ALU.subtract),
                  reads=[dt_], writes=[dr])
            kb.op("dve", lambda e: e.tensor_scalar(out=t_[R, :], in0=r_[R, :], scalar1=-0.5, scalar2=None, op0=ALU.is_lt),
                  reads=[dr], writes=[dt_])
            kb.op("dve", lambda e: e.tensor_tensor(out=r_[R, :], in0=r_[R, :], in1=t_[R, :], op=ALU.add),
                  reads=[dt_], writes=[dr])
            kb.op("act", lambda e, dst=dst: e.activation(out=dst[R, :], in_=r_[R, :], func=AF.Sin, scale=6.283185),
                  reads=[dr], partial=[drope])
        kb.op("dve", lambda e: e.tensor_scalar(out=sin2[R, :], in0=sin2[R, :], scalar1=sgn, scalar2=None, op0=ALU.mult),
              reads=[drope, self.dconst], writes=[drope])
        rs = self.scratch("rope", (2, 32, S), F32)
        sem2 = self.dma_sem("rope_out")
        kb.op("sp", lambda e: e.dma_start(out=rs[0, :, :], in_=cos2[R, :]), reads=[drope], dma=sem2)
        kb.op("sp", lambda e: e.dma_start(out=rs[1, :, :], in_=sin2[R, :]), reads=[drope], dma=sem2)
        kb.barrier()
        ar.release(m)

    def rope_chunks(self, b):
        kb, ar = self.kb, self.ar
        c32 = self.c32
        rs = self.scratch("rope", (2, 32, S), F32)
        R = slice(0, 32)
        invf = c32[R, CS["invf"]:CS["invf"] + 1]
        sgn = c32[R, CS["sgn"]:CS["sgn"] + 1]
        posi = ar.alloc((TT,), F32).bitcast(I32)
        y = ar.alloc((TT,), F32)
        r_ = ar.alloc((TT,), F32)
        t_ = ar.alloc((TT,), F32)
        outs = {"sin": ar.alloc((TT,), F32), "cos": ar.alloc((TT,), F32)}
        dp_, dy, dr, dt_, dout = Dep(), Dep(), Dep(), Dep(), {"sin": Dep(), "cos": Dep()}
        sem_i, sem_o = self.dma_sem("ropec_in"), self.dma_sem("ropec_out")

        def chunk(k4):
            cs = slice(k4 * TT, (k4 + 1) * TT)
            kb.op("sp", lambda e: e.dma_start(out=posi[R, :], in_=self.din("posb")[b, :, cs]), writes=[dp_], dma=sem_i)
            yield
            kb.op("pool", lambda e: e.tensor_copy(out=t_[R, :], in_=posi[R, :]), reads=[dp_], writes=[dt_])
            yield
            kb.op("pool", lambda e: e.tensor_scalar(out=y[R, :], in0=t_[R, :], scalar1=invf, scalar2=1.0 / (2.0 * math.pi),
                                                    op0=ALU.mult, op1=ALU.mult), reads=[dt_, self.dconst], writes=[dy])
            yield
            for which in ("sin", "cos"):
                dst, dd = outs[which], dout[which]
                if which == "cos":
                    kb.op("pool", lambda e: e.tensor_scalar(out=y[R, :], in0=y[R, :], scalar1=0.25, scalar2=None, op0=ALU.add),
                          reads=[dy], writes=[dy])
                    yield
                yi = r_.bitcast(I32)
                kb.op("pool", lambda e: e.tensor_copy(out=yi[R, :], in_=y[R, :]), reads=[dy], writes=[dr])
                yield
                kb.op("pool", lambda e: e.tensor_copy(out=t_[R, :], in_=yi[R, :]), reads=[dr], writes=[dt_])
                yield
                kb.op("pool", lambda e: e.tensor_tensor(out=r_[R, :], in0=y[R, :], in1=t_[R, :], op=ALU.subtract), reads=[dy, dt_], writes=[dr])
                yield
                kb.op("pool", lambda e: e.tensor_scalar(out=t_[R, :], in0=r_[R, :], scalar1=0.5, scalar2=None, op0=ALU.is_gt),
                      reads=[dr], writes=[dt_])
                yield
                kb.op("pool", lambda e: e.tensor_tensor(out=r_[R, :], in0=r_[R, :], in1=t_[R, :], op=ALU.subtract), reads=[dt_], writes=[dr])
                yield
                kb.op("pool", lambda e: e.tensor_scalar(out=t_[R, :], in0=r_[R, :], scalar1=-0.5, scalar2=None, op0=ALU.is_lt),
                      reads=[dr], writes=[dt_])
                yield
                kb.op("pool", lambda e: e.tensor_tensor(out=r_[R, :], in0=r_[R, :], in1=t_[R, :], op=ALU.add), reads=[dt_], writes=[dr])
                yield
                kb.op("act", lambda e: e.activation(out=dst[R, :], in_=r_[R, :], func=AF.Sin, scale=6.283185), reads=[dr], writes=[dd])
                yield
            kb.op("pool", lambda e: e.tensor_scalar(out=outs["sin"][R, :], in0=outs["sin"][R, :], scalar1=sgn, scalar2=None, op0=ALU.mult),
                  reads=[self.dconst], writes=[dout["sin"]])
            yield
            kb.op("sp", lambda e: e.dma_start(out=rs[0, :, cs], in_=outs["cos"][R, :]), reads=[dout["cos"]], dma=sem_o)
            yield
            kb.op("sp", lambda e: e.dma_start(out=rs[1, :, cs], in_=outs["sin"][R, :]), reads=[dout["sin"]], dma=sem_o)
            yield

        def gen():
            for k4 in range(NTT):
                yield from chunk(k4)
        return gen()

    def rope_tables(self, b):
        kb, ar = self.kb, self.ar
        rs = self.scratch("rope", (2, 32, S), F32)
        cos2 = ar.alloc((S,), F32)
        sin2 = ar.alloc((S,), F32)
        drope = Dep()
        sem = self.dma_sem("rope_in")
        R = slice(0, 32)
        kb.op("sp", lambda e: e.dma_start(out=cos2[R, :], in_=rs[0, :, :]), partial=[drope], dma=sem)
        kb.op("sp", lambda e: e.dma_start(out=sin2[R, :], in_=rs[1, :, :]), partial=[drope], dma=sem)
        self.cos2, self.sin2, self.drope = cos2, sin2, drope


    def head_bufs(self):
        ar = self.ar
        return {
            "nb": self.norm_bufs(TT),
            "kn": Ring([(ar.alloc((TT,), BF16), Dep()) for _ in range(4)]),
            "sq2": Ring([(ar.alloc((TT,), BF16), Dep()) for _ in range(3)]),
            "t1": Ring([(ar.alloc((TT,), F32), Dep()) for _ in range(2)]),
            "t2": Ring([(ar.alloc((TT,), F32), Dep()) for _ in range(3)]),
        }

    def head_stage0b(self, it, hb):
        kb = self.kb
        sq, dsq = hb["sq2"].next()
        kb.op("act", lambda e: e.activation(out=sq, in_=it["pt"][:, :], func=AF.Square), reads=[it["dp"]], writes=[dsq])
        it["sq"], it["dsq"] = sq, dsq

    def head_stage1(self, it, hb):
        kb = self.kb
        pt, dp = it["pt"], it["dp"]
        pss, dps = self.ps("aux")
        kb.op("pe", lambda e: e.matmul(pss[:, :], lhsT=self.ones_bf, rhs=it["sq"], start=True, stop=True),
              reads=[it["dsq"], self.dconst], writes=[dps])
        lnv, dln = hb["nb"]["lnv"].next()
        rstd, drs = hb["nb"]["rstd"].next()
        kb.op("act", lambda e: e.activation(out=lnv, in_=pss[:, :], func=AF.Ln, scale=1.0 / 128.0, bias=self.epsb),
              reads=[dps, self.dconst], writes=[dln])
        kb.op("act", lambda e: e.activation(out=rstd, in_=lnv, func=AF.Exp, scale=-0.5,
                                            bias=(self.lnsb if it["extra"] != 1.0 else self.zerob)),
              reads=[dln, self.dconst], writes=[drs])
        kn, dkn = hb["kn"].next()
        kb.op("dve", lambda e: e.scalar_tensor_tensor(out=kn, in0=pt[:, :], scalar=it["gcol"], in1=rstd[:, 0:TT], op0=ALU.mult, op1=ALU.mult),
              reads=[dp, drs, self.dconst], writes=[dkn])
        it["kn"], it["dkn"] = kn, dkn

    def head_stage2(self, it, hb):
        kb = self.kb
        kn, dkn, tt, d, dst, ddst = it["kn"], it["dkn"], it["tt"], it["d"], it["dst"], it["ddst"]
        cs = slice(tt * TT, (tt + 1) * TT)
        R = slice(0, 32)
        psw, dpsw = self.ps("aux")
        kb.op("pe", lambda e: e.matmul(psw[:, :], lhsT=self.cb[:, CS["perm"]:CS["perm"] + 128], rhs=kn, start=True, stop=True),
              reads=[dkn, self.dconst], writes=[dpsw])
        t1, dt1 = hb["t1"].next()
        t2, dt2 = hb["t2"].next()
        kb.op("pool", lambda e: e.tensor_tensor(out=t1[R, :], in0=kn[R, :], in1=self.cos2[R, cs], op=ALU.mult),
              reads=[dkn, self.drope], writes=[dt1])
        kb.op("dve", lambda e: e.tensor_tensor(out=t2[R, :], in0=psw[R, :], in1=self.sin2[R, cs], op=ALU.mult),
              reads=[dpsw, self.drope], writes=[dt2])
        nj = TT // d
        j0 = tt * nj
        if d == 1:
            dv_all = dst[:, cs]
            dv_rot = dst[R, cs]
            src_all, s1, s2 = kn, t1[R, :], t2[R, :]
        else:
            dvw = dst.rearrange("p (r j) -> p r j", r=d)
            dv_all = dvw[:, :, j0:j0 + nj]
            dv_rot = dvw[R, :, j0:j0 + nj]
            src_all = kn.rearrange("p (j r) -> p r j", r=d)
            s1 = t1[R, :].rearrange("p (j r) -> p r j", r=d)
            s2 = t2[R, :].rearrange("p (j r) -> p r j", r=d)
        kb.op("act", lambda e: e.activation(out=dv_all, in_=src_all, func=AF.Identity), reads=[dkn], writes=[ddst[tt]])
        kb.op("dve", lambda e: e.tensor_tensor(out=dv_rot, in0=s1, in1=s2, op=ALU.add), reads=[dt1, dt2], writes=[ddst[tt]])
        if it.get("done") is not None:
            it["done"]()

    def head_pipeline(self, items, stage0, hb):
        n = len(items)
        for step in range(n + 2):
            if step < n:
                stage0(items[step])
                self.head_stage0b(items[step], hb)
            if 0 <= step - 1 < n:
                self.head_stage1(items[step - 1], hb)
            if 0 <= step - 2 < n:
                self.head_stage2(items[step - 2], hb)

    def kv(self, b):
        kb, ar = self.kb, self.ar
        self.small_consts()
        Kc = self.scratch("Kc", (24, 128, S), BF16)
        Vc = self.scratch("Vc", (3, 128, 16, 1024), BF16)
        if not self.rope_ready:
            self.rope_compute(b)
        self.rope_tables(b)
        hb = self.head_bufs()
        kvn = ar.alloc((KC, S), BF16)
        dkvn = [[Dep() for _ in range(NTT)] for _ in range(KC)]
        wks = [(ar.alloc((KC, 128), BF16), Dep(), self.dma_sem(f"wk{i}_{kb.nops}")) for i in range(3)]
        wvs2 = [(ar.alloc((KC, 1024), BF16), Dep(), self.dma_sem(f"wv{i}_{kb.nops}")) for i in range(2)]
        kbf = [(ar.alloc((S,), BF16), [Dep() for _ in range(NTT)], self.dma_sem(f"kbf{i}_{kb.nops}")) for i in range(2)]
        vts = [(ar.alloc((1024,), BF16), Dep(), self.dma_sem(f"vt{i}_{kb.nops}")) for i in range(2)]
        wk_h, wv_h = self.din("wk"), self.din("wv")

        def load_wk(gh):
            w, dw, sem = wks[gh % 3]
            kb.op("pool", lambda e: e.dma_start(out=w, in_=wk_h[gh, :, :, :], max_dma_last_dim=4096), writes=[dw], dma=sem)

        def load_wv(g):
            wv_, dwv, semv = wvs2[g % 2]
            for kc in range(KC):
                kb.op("pool", lambda e: e.dma_start(out=wv_[:, kc, :], in_=wv_h[g, :, kc, :], max_dma_last_dim=4096),
                      writes=[dwv] if kc == 0 else [], partial=[] if kc == 0 else [dwv], dma=semv)

        load_wk(0)
        load_wk(1)
        load_wv(0)
        load_wv(1)
        need_norm = self.lazy_norm(SM["kv_norm"], kvn, dkvn, hb["nb"])
        items = []
        for gh in range(24):
            g = gh // 8
            dst, ddst, sem = kbf[gh % 2]
            for tt in range(NTT):
                it = dict(gh=gh, tt=tt, d=DIL[g], dst=dst, ddst=ddst, extra=1.0,
                          gcol=self.sm[:, SM["k_norm"] + g:SM["k_norm"] + g + 1], done=None)
                if tt == NTT - 1:
                    it["done"] = (lambda gh=gh, dst=dst, ddst=ddst, sem=sem: kb.op(
                        "sp", lambda e: e.dma_start(out=Kc[gh, :, :], in_=dst), reads=ddst, dma=sem))
                items.append(it)

        def stage0(it):
            gh, tt = it["gh"], it["tt"]
            if tt == 0 and gh + 2 < 24:
                load_wk(gh + 2)
            w, dw, _ = wks[gh % 3]
            need_norm(tt)
            cs = slice(tt * TT, (tt + 1) * TT)
            pt, dp = self.ps("mm")
            for kc in range(KC):
                kb.op("pe", lambda e: e.matmul(pt[:, :], lhsT=w[:, kc, :], rhs=kvn[:, kc, cs], start=(kc == 0), stop=(kc == KC - 1)),
                      reads=[dw, dkvn[kc][tt]], partial=[dp], flag=(kc == KC - 1))
            it["pt"], it["dp"] = pt, dp

        self.head_pipeline(items, stage0, hb)
        for g in range(3):
            d = DIL[g]
            nbk = (S // d) // 128
            if g == 1:
                load_wv(2)
            wv_, dwv, semv = wvs2[g % 2]
            for blk in range(16):
                r, n = blk // nbk, blk % nbk
                t0 = r + n * 128 * d
                tsl = slice(t0, t0 + 127 * d + 1, d)
                tts = sorted(set([t0 // TT, (t0 + 127 * d) // TT]))
                vt, dvt, semt = vts[blk % 2]
                for i in range(2):
                    pv, dpv = self.ps("mm")
                    for kc in range(KC):
                        kb.op("pe", lambda e: e.matmul(pv[:, :], lhsT=kvn[:, kc, tsl], rhs=wv_[:, kc, i * 512:(i + 1) * 512],
                                                       start=(kc == 0), stop=(kc == KC - 1)),
                              reads=[dwv] + [dkvn[kc][t_] for t_ in tts], partial=[dpv], flag=(kc == KC - 1))
                    if i == 0:
                        kb.op("act", lambda e: e.activation(out=vt[:, 0:512], in_=pv[:, :], func=AF.Identity), reads=[dpv], writes=[dvt])
                    else:
                        kb.op("dve", lambda e: e.tensor_copy(out=vt[:, 512:1024], in_=pv[:, :]), reads=[dpv], partial=[dvt])
                kb.op("sp", lambda e: e.dma_start(out=Vc[g, :, blk, :], in_=vt), reads=[dvt], dma=semt)

    def dil(self, J, b):
        kb, ar = self.kb, self.ar
        self.small_consts()
        Kc = self.scratch("Kc", (24, 128, S), BF16)
        Vc = self.scratch("Vc", (3, 128, 16, 1024), BF16)
        Qc = self.scratch("Qc", (24, 128, S), BF16)
        wq_h, wo_h = self.din(f"wq{J}"), self.din(f"bwo{J}")
        m_q = ar.mark()
        self.rope_tables(b)
        hb = self.head_bufs()
        xn = ar.alloc((KC, S), BF16)
        dxn = [[Dep() for _ in range(NTT)] for _ in range(KC)]
        wqs = [(ar.alloc((KC, 128), BF16), Dep(), self.dma_sem(f"wq{i}_{kb.nops}")) for i in range(3)]
        qbf = [(ar.alloc((S,), BF16), [Dep() for _ in range(NTT)], self.dma_sem(f"qbf{i}_{kb.nops}")) for i in range(2)]

        def load_wq(gh):
            w, dw, sem = wqs[gh % 3]
            kb.op("pool", lambda e: e.dma_start(out=w, in_=wq_h[gh, :, :, :], max_dma_last_dim=4096), writes=[dw], dma=sem)

        load_wq(0)
        load_wq(1)
        need_norm = self.lazy_norm(SM["b_norm"] + J * 8, xn, dxn, hb["nb"])
        items = []
        for gh in range(24):
            g = gh // 8
            dst, ddst, sem = qbf[gh % 2]
            for tt in range(NTT):
                it = dict(gh=gh, tt=tt, d=DIL[g], dst=dst, ddst=ddst, extra=128.0 ** -0.5,
                          gcol=self.sm[:, SM["q_norm"] + J * 3 + g:SM["q_norm"] + J * 3 + g + 1], done=None)
                if tt == NTT - 1:
                    it["done"] = (lambda gh=gh, dst=dst, ddst=ddst, sem=sem: kb.op(
                        "sp", lambda e: e.dma_start(out=Qc[gh, :, :], in_=dst), reads=ddst, dma=sem))
                items.append(it)

        def stage0(it):
            gh, tt = it["gh"], it["tt"]
            if tt == 0 and gh + 2 < 24:
                load_wq(gh + 2)
            w, dw, _ = wqs[gh % 3]
            need_norm(tt)
            cs = slice(tt * TT, (tt + 1) * TT)
            pt, dp = self.ps("mm")
            for kc in range(KC):
                kb.op("pe", lambda e: e.matmul(pt[:, :], lhsT=w[:, kc, :], rhs=xn[:, kc, cs], start=(kc == 0), stop=(kc == KC - 1)),
                      reads=[dw, dxn[kc][tt]], partial=[dp], flag=(kc == KC - 1))
            it["pt"], it["dp"] = pt, dp

        self.head_pipeline(items, stage0, hb)
        kb.barrier()
        ar.release(m_q)
        ob = ar.alloc((KC, S), BF16)
        dob = [Dep() for _ in range(KC)]
        NDs = [(ar.alloc((2, S), F32), Dep()) for _ in range(2)]
        NSL = 3
        slots = [dict(K=ar.alloc((S,), BF16), Q=ar.alloc((S,), BF16), V=ar.alloc((16, 128), BF16), d=Dep(),
                      sem=self.dma_sem(f"att{i}_{kb.nops}")) for i in range(NSL)]
        pts = Ring([(ar.alloc((256,), BF16), Dep()) for _ in range(8)])
        wo = ar.alloc((KC, D), BF16)
        dwo = Dep()
        semwo = self.dma_sem(f"bwo_{kb.nops}")
        for kc in range(KC):
            kb.op("pool", lambda e: e.dma_start(out=wo[:, kc, :], in_=wo_h[:, kc, :], max_dma_last_dim=4096), partial=[dwo], dma=semwo)
        m2 = self.cb[:, CS["m2"]:CS["m2"] + 256]
        ident = self.cb[:, CS["ident"]:CS["ident"] + 128]
        combos = [(h, g) for h in range(8) for g in range(3)]
        deferred = []

        def load(ci):
            h, g = combos[ci]
            gh = g * 8 + h
            sl = slots[ci % NSL]
            kb.op("sp", lambda e: e.dma_start(out=sl["K"], in_=Kc[gh, :, :]), writes=[sl["d"]], dma=sl["sem"])
            kb.op("sp", lambda e: e.dma_start(out=sl["Q"], in_=Qc[gh, :, :]), partial=[sl["d"]], dma=sl["sem"])
            kb.op("sp", lambda e: e.dma_start(out=sl["V"], in_=Vc[g, :, :, h * 128:(h + 1) * 128]), partial=[sl["d"]], dma=sl["sem"])

        load(0)
        load(1)
        for ci, (h, g) in enumerate(combos):
            if ci + 2 < len(combos):
                load(ci + 2)
            sl = slots[ci % NSL]
            Kt, Qt, Vt, dsl = sl["K"], sl["Q"], sl["V"], sl["d"]
            ND, dND = NDs[h % 2]
            d = DIL[g]
            nbk = (S // d) // 128
            blocks = []
            for r in range(d):
                for n in range(nbk):
                    blocks.append(dict(r=r, n=n, blk=r * nbk + n, N=(256 if n < nbk - 1 else 128)))

            def att_s0(bi):
                bd = blocks[bi]
                c0, N = bd["blk"] * 128, bd["N"]
                psc, dpsc = self.ps("mm")
                kb.op("pe", lambda e: e.matmul(psc[:, 0:N], lhsT=Kt[:, c0:c0 + 128], rhs=Qt[:, c0:c0 + N], start=True, stop=False),
                      reads=[dsl], partial=[dpsc], flag=False)
                kb.op("pe", lambda e: e.matmul(psc[:, 0:N], lhsT=ident, rhs=m2[:, 0:N], start=False, stop=True),
                      reads=[self.dconst], partial=[dpsc], flag=True)
                pT, dpT = pts.next()
                kb.op("act", lambda e: e.activation(out=pT[:, 0:N], in_=psc[:, 0:N], func=AF.Exp), reads=[dpsc], writes=[dpT])
                bd["pT"], bd["dpT"] = pT, dpT

            def att_s1(bi):
                bd = blocks[bi]
                r, n, blk = bd["r"], bd["n"], bd["blk"]
                pT, dpT = bd["pT"], bd["dpT"]
                prev = (blocks[bi - 1]["pT"], blocks[bi - 1]["dpT"]) if n > 0 else None
                po, dpo = self.ps("aux")
                rd = [dsl, dpT] + ([prev[1]] if prev is not None else [])
                kb.op("pe", lambda e: e.matmul(po[:, 0:128], lhsT=Vt[:, blk, :], rhs=pT[:, 0:128], start=True, stop=(prev is None)),
                      reads=rd, partial=[dpo], flag=False)
                if prev is not None:
                    kb.op("pe", lambda e: e.matmul(po[:, 0:128], lhsT=Vt[:, blk - 1, :], rhs=prev[0][:, 128:256], start=False, stop=True),
                          reads=rd, partial=[dpo], flag=False)
                kb.op("pe", lambda e: e.matmul(po[:, 128:256], lhsT=self.ones_bf, rhs=pT[:, 0:128], start=True, stop=(prev is None)),
                      reads=rd + [self.dconst], partial=[dpo], flag=(prev is None))
                if prev is not None:
                    kb.op("pe", lambda e: e.matmul(po[:, 128:256], lhsT=self.ones_bf, rhs=prev[0][:, 128:256], start=False, stop=True),
                          reads=rd + [self.dconst], partial=[dpo], flag=True)
                t0 = r + n * 128 * d
                ndv = ND[:, :, t0:t0 + 127 * d + 1:d]
                pov = po[:, 0:256].rearrange("p (a c) -> p a c", a=2)
                if g == 0:
                    kb.op("dve", lambda e: e.tensor_copy(out=ndv, in_=pov), reads=[dpo], writes=[dND])
                else:
                    kb.op("dve", lambda e: e.tensor_tensor(out=ndv, in0=pov, in1=ndv, op=ALU.add), reads=[dpo], writes=[dND])

            SK = 3
            for step in range(len(blocks) + SK):
                if step < len(blocks):
                    att_s0(step)
                if step - SK >= 0:
                    att_s1(step - SK)
                if deferred and step % 2 == 1:
                    deferred.pop(0)()
            if g == 2:
                for k4 in range(NTT):
                    def fin(ND=ND, dND=dND, h=h, k4=k4):
                        fc = slice(k4 * TT, (k4 + 1) * TT)
                        kb.op("act", lambda e: e.activation(out=ND[:, 1, fc], in_=ND[:, 1, fc], func=AF.Ln), reads=[], writes=[dND])
                        kb.op("act", lambda e: e.activation(out=ND[:, 1, fc], in_=ND[:, 1, fc], func=AF.Exp, scale=-1.0), reads=[], writes=[dND])
                        kb.op("dve", lambda e: e.tensor_tensor(out=ob[:, h, fc], in0=ND[:, 0, fc], in1=ND[:, 1, fc], op=ALU.mult),
                              reads=[dND], partial=[dob[h]])
                    deferred.append(fin)
        while deferred:
            deferred.pop(0)()
        for tt in range(NTT):
            cs = slice(tt * TT, (tt + 1) * TT)
            for c in range(KC):
                pw, dpw = self.ps("mm")
                for h in range(KC):
                    kb.op("pe", lambda e: e.matmul(pw[:, :], lhsT=wo[:, h, c * 128:(c + 1) * 128], rhs=ob[:, h, cs],
                                                   start=(h == 0), stop=(h == KC - 1)),
                          reads=[dwo, dob[h]], partial=[dpw], flag=(h == KC - 1))
                kb.op("dve", lambda e: e.tensor_tensor(out=self.x[:, c, cs], in0=pw[:, :], in1=self.x[:, c, cs], op=ALU.add),
                      reads=[dpw], writes=[self.dx[c][tt]])

def prep_weights(inp):
    f = np.float32
    sm = np.zeros((128, NSMALL), f)

    def put(name, arr):
        sm[:, SM[name]:SM[name] + arr.shape[1]] = arr

    put("a_norm", inp["a_norm"].reshape(2, 8, 128).transpose(2, 0, 1).reshape(128, 16))
    put("f_norm", inp["f_norm"].reshape(4, 8, 128).transpose(2, 0, 1).reshape(128, 32))
    put("kv_norm", inp["kv_norm"].reshape(8, 128).T)
    put("b_norm", inp["b_norm"].reshape(2, 8, 128).transpose(2, 0, 1).reshape(128, 16))
    put("k_norm", inp["k_norm"].T)
    put("q_norm", inp["q_norm"].reshape(6, 128).T)
    put("o_norm", inp["a_out_norm"].reshape(2, 2, 128).transpose(2, 0, 1).reshape(128, 4))
    cv = np.concatenate([inp["f_conv"], inp["f_conv_b"][:, None, :]], axis=1)
    put("conv", cv.reshape(4, 4, 44, 128).transpose(3, 0, 2, 1).reshape(128, 4 * 44 * 4))
    w = {"small": sm, "cst": make_consts()}
    for l in range(2):
        wg = np.zeros((33, 512), f)
        wg[:16] = inp["a_w_gate"][l]
        wg[32] = inp["a_b_gate"][l]
        w[f"wg{l}"] = wg
        w[f"win{l}"] = np.ascontiguousarray(inp["a_w_in"][l].reshape(8, 128, INW).transpose(1, 0, 2))
        w[f"awo{l}"] = np.ascontiguousarray(inp["a_w_out"][l].reshape(8, 128, D).transpose(1, 0, 2))
        w[f"wq{l}"] = np.ascontiguousarray(inp["b_w_q"][l].reshape(8, 128, 24, 128).transpose(2, 1, 0, 3))
        w[f"bwo{l}"] = np.ascontiguousarray(inp["b_w_out"][l].reshape(8, 128, D).transpose(1, 0, 2))
    wkv = inp["w_kv"]
    w["wk"] = np.ascontiguousarray(wkv[:, :3072].reshape(8, 128, 24, 128).transpose(2, 1, 0, 3))
    w["wv"] = np.ascontiguousarray(wkv[:, 3072:].reshape(8, 128, 3, 1024).transpose(2, 1, 0, 3))
    for l in range(4):
        up = inp["f_w_up"][l].reshape(8, 128, 2, NJ, 128)
        w[f"wup{l}"] = np.ascontiguousarray(up.transpose(3, 1, 2, 0, 4)).reshape(NJ, 128, 2 * KC * 128)
        w[f"wdn{l}"] = np.ascontiguousarray(inp["f_w_down"][l].reshape(NJ, 128, D).transpose(1, 0, 2))
    return w


FULL_PHASES = [("gla", 0), ("ffn", 0), ("gla", 1), ("ffn", 1), ("kv",), ("dil", 0), ("ffn", 2), ("dil", 1), ("ffn", 3)]
_CACHE = {}
FUSED = True


LAST_EXEC_NS = [None]


def run_phases(phases, x, positions, w, nseq=NSEQ, trace=False):
    key = (tuple(phases), nseq)
    if key not in _CACHE:
        _CACHE[key] = Prog(phases, nseq)
    prog = _CACHE[key]
    B = x.shape[0]
    ncore = B // nseq
    xT = np.ascontiguousarray(x.reshape(ncore, nseq, S, KC, 128).transpose(0, 1, 4, 3, 2))
    posb = np.ascontiguousarray(np.broadcast_to(positions.reshape(ncore, nseq, 1, S), (ncore, nseq, 32, S))).astype(np.int32)
    in_maps = []
    for i in range(ncore):
        m = {"small": w["small"], "cst": w["cst"], "xT": xT[i]}
        for name in prog.inputs:
            m[name] = posb[i] if name == "posb" else w[name]
        in_maps.append(m)
    if trace:
        res = run_bass_kernel_spmd(prog.nc, in_maps, core_ids=list(range(ncore)), trace=True)
        LAST_EXEC_NS[0] = res.exec_time_ns
    else:
        res = run_bass_kernel_spmd(prog.nc, in_maps, core_ids=list(range(ncore)))
    yT = np.stack([r["yT"] for r in res.results])
    return np.ascontiguousarray(yT.transpose(0, 1, 4, 3, 2)).reshape(B, S, D)


def kernel(**inputs):
    inp = {k: np.asarray(v) for k, v in inputs.items()}
    w = prep_weights(inp)
    x = inp["x"].astype(np.float32, copy=False)
    if FUSED:
        return run_phases(FULL_PHASES, x, inp["positions"], w)
    x = run_phases(FULL_PHASES[:4], x, inp["positions"], w)
    return run_phases(FULL_PHASES[4:], x, inp["positions"], w)
```

```python
import contextlib
import math
import numpy as np
import concourse.bass as bass
import concourse.mybir as mybir
from concourse.bass_utils import run_bass_kernel_spmd

F32 = mybir.dt.float32
BF16 = mybir.dt.bfloat16
I32 = mybir.dt.int32
AF = mybir.ActivationFunctionType
ALU = mybir.AluOpType

D = 1024
S = 2048
KC = 8
NSEQ = 2
DFF = 2816
NJ = 22
EPS = 1e-6
INW = 3088
TT = 512
NTT = S // TT
DIL = (1, 4, 16)
GCH = 128


class Dep:
    __slots__ = ("w", "r", "war")

    def __init__(self):
        self.w = {}
        self.r = {}
        self.war = {}


def _merge(dst, src):
    for k, v in src.items():
        if dst.get(k, 0) < v:
            dst[k] = v


class _Rec:
    def __init__(self):
        self.call = None

    def __getattr__(self, name):
        def f(*a, **k):
            assert self.call is None
            self.call = (name, a, k)
            return self
        return f


class KB:
    ENGS = ("pe", "act", "dve", "pool", "sp")

    def __init__(self, nc, stack):
        self.nc = nc
        self.stack = stack
        self.prog = {e: [] for e in self.ENGS}
        self.cnt = {}
        self.known = {e: {} for e in self.ENGS}
        self.semh = {}
        self.esem = {}
        for e in self.ENGS:
            if e != "sp":
                self.esem[e] = self.new_sem("s_" + e)
        self.pending_pe = False
        self.nops = 0

    def new_sem(self, name):
        h = self.stack.enter_context(self.nc.semaphore(name))
        self.semh[name] = h
        self.cnt[name] = 0
        return name

    def _emit_waits(self, e, waits):
        kn = self.known[e]
        for s, v in waits.items():
            if e == "pe" and self.esem.get(e) == s:
                continue
            if kn.get(s, 0) < v:
                kn[s] = v
                self.prog[e].append(("wait", s, v))

    def op(self, e, fn, reads=(), writes=(), partial=(), flag=True, dma=None):
        waits = {}
        for d in reads:
            _merge(waits, d.w)
        for d in writes:
            _merge(waits, d.w)
            _merge(waits, d.r)
            _merge(waits, d.war)
        for d in partial:
            _merge(waits, d.war)
            if d.r:
                _merge(waits, d.r)
                _merge(waits, d.w)
        self._emit_waits(e, waits)
        rec = _Rec()
        fn(rec)
        fn = rec.call
        if dma is not None:
            self.cnt[dma] += 16
            ev = (dma, self.cnt[dma])
            self.prog[e].append(("op", fn, dma, 16))
        else:
            s = self.esem[e]
            if flag:
                self.cnt[s] += 1
                ev = (s, self.cnt[s])
                self.prog[e].append(("op", fn, s, 1))
                if e == "pe":
                    self.pending_pe = False
            else:
                assert e == "pe"
                ev = (s, self.cnt[s] + 1)
                self.prog[e].append(("op", fn, None, 0))
                self.pending_pe = True
        evd = {ev[0]: ev[1]}
        for d in reads:
            _merge(d.r, evd)
        for d in writes:
            nw = {}
            _merge(nw, d.w)
            _merge(nw, d.r)
            d.war = nw
            d.w = dict(evd)
            d.r = {}
        for d in partial:
            if d.r:
                nw = {}
                _merge(nw, d.w)
                _merge(nw, d.r)
                _merge(nw, d.war)
                d.war = nw
                d.w = {}
                d.r = {}
            _merge(d.w, evd)
        self.nops += 1
        return ev

    def barrier(self):
        assert not self.pending_pe
        allv = {s: v for s, v in self.cnt.items() if v > 0}
        for e in self.ENGS:
            self._emit_waits(e, allv)

    def check(self):
        assert not self.pending_pe, "trailing unflagged PE instruction"
        val = {s: 0 for s in self.cnt}
        pc = {e: 0 for e in self.ENGS}
        progress = True
        while progress:
            progress = False
            for e in self.ENGS:
                p = self.prog[e]
                while pc[e] < len(p):
                    it = p[pc[e]]
                    if it[0] == "wait":
                        if val[it[1]] >= it[2]:
                            pc[e] += 1
                            progress = True
                        else:
                            break
                    else:
                        if it[2] is not None:
                            val[it[2]] += it[3]
                        pc[e] += 1
                        progress = True
        for e in self.ENGS:
            if pc[e] < len(self.prog[e]):
                raise RuntimeError(f"deadlock: {e} stuck at {pc[e]}/{len(self.prog[e])}: {self.prog[e][pc[e]][:3]}")

    def finish(self):
        self.barrier()
        self.check()
        nc = self.nc
        prog = self.prog
        semh = self.semh

        def replay(e):
            def body(engine):
                for it in prog[e]:
                    if it[0] == "wait":
                        engine.wait_ge(semh[it[1]], it[2])
                    else:
                        name, a, k = it[1]
                        ins = getattr(engine, name)(*a, **k)
                        if it[2] is not None:
                            ins.then_inc(semh[it[2]], it[3])
            return body

        with nc.Block() as block:
            block.tensor(replay("pe"))
            block.scalar(replay("act"))
            block.vector(replay("dve"))
            block.gpsimd(replay("pool"))
            block.sync(replay("sp"))


class Arena:
    def __init__(self, nc, stack, nbytes):
        self.words = nbytes // 4
        self.t = stack.enter_context(nc.sbuf_tensor("arena", [128, self.words], F32))
        self.off = 0
        self.peak = 0

    def alloc(self, shape, dt):
        n = int(np.prod(shape))
        esz = 2 if dt == BF16 else 4
        words = (n * esz + 3) // 4
        words = (words + 15) // 16 * 16
        if self.off + words > self.words:
            raise RuntimeError(f"arena overflow: need {self.off + words} words of {self.words}")
        ap = self.t[:, self.off:self.off + words]
        self.off += words
        self.peak = max(self.peak, self.off)
        if dt != F32:
            ap = ap.bitcast(dt)
        ap = ap[:, 0:n]
        if len(shape) == 2:
            ap = ap.rearrange("p (a b) -> p a b", a=shape[0])
        elif len(shape) == 3:
            ap = ap.rearrange("p (a b c) -> p a b c", a=shape[0], b=shape[1])
        return ap

    def mark(self):
        return self.off

    def release(self, m):
        self.off = m


class Ring:
    def __init__(self, items):
        self.items = items
        self.i = 0

    def next(self):
        it = self.items[self.i % len(self.items)]
        self.i += 1
        return it


SM = {}
_o = 0
for _n, _w in (("a_norm", 2 * 8), ("f_norm", 4 * 8), ("kv_norm", 8), ("b_norm", 2 * 8), ("k_norm", 3),
               ("q_norm", 6), ("o_norm", 4), ("conv", 4 * 44 * 4)):
    SM[_n] = _o
    _o += _w
NSMALL = _o

CS = {}
_o = 0
for _n, _w in (("ones", 128), ("tri", 128), ("ut", 128), ("bd", 128), ("perm", 128), ("invf", 1), ("sgn", 1),
               ("m2", 256), ("ident", 128), ("bd4", 512), ("eps", 1), ("zero", 1), ("lns", 1), ("one", 1)):
    CS[_n] = _o
    _o += _w
NCST = _o


def make_consts():
    c = np.zeros((128, NCST), np.float32)
    c[:, CS["ones"]:CS["ones"] + 128] = 1.0
    m = np.arange(128)[:, None]
    q = np.arange(128)[None, :]
    same = (m // GCH) == (q // GCH)
    c[:, CS["tri"]:CS["tri"] + 128] = np.where(same & (m <= q), -1.0 / 16.0, 0.0)
    c[:, CS["ut"]:CS["ut"] + 128] = np.where(same & (m > q), -1.0 / 16.0, 0.0)
    c[:, CS["bd"]:CS["bd"] + 128] = np.where(same & (m <= q), 1.0, 0.0)
    k32 = np.arange(32)[:, None]
    m32 = np.arange(32)[None, :]
    pm = np.eye(128, dtype=np.float32)
    pm[:32, :32] = (k32 == (m32 + 16) % 32).astype(np.float32)
    c[:, CS["perm"]:CS["perm"] + 128] = pm
    invf = 500000.0 ** (-(np.arange(16, dtype=np.float64) * 2.0 / 32.0))
    c[:32, CS["invf"]] = np.concatenate([invf, invf]).astype(np.float32)
    c[:32, CS["sgn"]] = np.concatenate([-np.ones(16), np.ones(16)]).astype(np.float32)
    c[:, CS["m2"]:CS["m2"] + 128] = np.where(m <= q, 0.0, -30000.0)
    c[:, CS["m2"] + 128:CS["m2"] + 256] = np.where(m >= q, 0.0, -30000.0)
    c[:, CS["ident"]:CS["ident"] + 128] = np.eye(128, dtype=np.float32)
    c[:, CS["bd4"]:CS["bd4"] + 512] = np.tile(np.where(same & (m <= q), 1.0, 0.0), (1, 4))
    c[:, CS["eps"]] = EPS
    c[:, CS["zero"]] = 0.0
    c[:, CS["lns"]] = math.log(128.0 ** -0.5)
    c[:, CS["one"]] = 1.0
    return c


IN_SHAPES = {"posb": ((NSEQ, 32, S), I32), "wk": ((24, 128, KC, 128), F32), "wv": ((3, 128, KC, 1024), F32)}
for _l in range(2):
    IN_SHAPES[f"wg{_l}"] = ((33, 512), F32)
    IN_SHAPES[f"win{_l}"] = ((128, KC, INW), F32)
    IN_SHAPES[f"awo{_l}"] = ((128, KC, D), F32)
    IN_SHAPES[f"wq{_l}"] = ((24, 128, KC, 128), F32)
    IN_SHAPES[f"bwo{_l}"] = ((128, KC, D), F32)
for _l in range(4):
    IN_SHAPES[f"wup{_l}"] = ((NJ, 128, 2 * KC * 128), F32)
    IN_SHAPES[f"wdn{_l}"] = ((128, NJ, D), F32)


class Prog:
    def __init__(self, phases, nseq=NSEQ, arena_kib=206):
        self.phases = phases
        self.nseq = nseq
        nc = bass.Bass("TRN2", target_bir_lowering=False)
        self.nc = nc
        dt = nc.dram_tensor
        self.xT = dt("xT", [nseq, 128, KC, S], F32, kind="ExternalInput").ap()
        self.yT = dt("yT", [nseq, 128, KC, S], F32, kind="ExternalOutput").ap()
        self.small = dt("small", [128, NSMALL], F32, kind="ExternalInput").ap()
        self.cst = dt("cst", [128, NCST], F32, kind="ExternalInput").ap()
        self.inputs = {}
        self._scratch = {}

        with contextlib.ExitStack() as st:
            kb = KB(nc, st)
            self.kb = kb
            self.ar = Arena(nc, st, arena_kib * 1024)
            self.pbank = [st.enter_context(nc.psum_tensor(f"pb{i}", [128, 512], F32)) for i in range(8)]
            self.pdep = [Dep() for _ in range(8)]
            self.ring_mm = Ring([0, 1, 2, 3])
            self.ring_aux = Ring([4, 5, 6, 7])
            self.ring_all = Ring([0, 1, 2, 3, 4, 5, 6, 7])
            self.sem_pool, self.phase_sems, self.nsem = [], [], 0
            self.sem_c = kb.new_sem("d_const")
            self.sem_x = [kb.new_sem(f"d_x{c}") for c in range(KC)]
            self.sem_out = kb.new_sem("d_out")
            self.build()
            kb.finish()

    def din(self, name):
        if name not in self.inputs:
            shp = list(IN_SHAPES[name][0])
            if name == "posb":
                shp[0] = self.nseq
            self.inputs[name] = self.nc.dram_tensor(name, shp, IN_SHAPES[name][1], kind="ExternalInput").ap()
        return self.inputs[name]

    def scratch(self, name, shape, dt):
        if name not in self._scratch:
            self._scratch[name] = self.nc.dram_tensor(name, list(shape), dt, kind="Internal").ap()
        return self._scratch[name]

    def ps(self, ring="mm"):
        i = {"mm": self.ring_mm, "aux": self.ring_aux, "all": self.ring_all}[ring].next()
        return self.pbank[i], self.pdep[i]

    def dma_sem(self, name):
        if self.sem_pool:
            sname = self.sem_pool.pop()
        else:
            sname = self.kb.new_sem(f"dq{self.nsem}")
            self.nsem += 1
        self.phase_sems.append(sname)
        return sname

    def build(self):
        kb, ar = self.kb, self.ar
        self.c32 = ar.alloc((NCST,), F32)
        self.sm = ar.alloc((NSMALL,), F32)
        self.cb = ar.alloc((NCST,), BF16)
        self.x = ar.alloc((KC, S), F32)
        self.dconst = Dep()
        self.dx = [[Dep() for _ in range(NTT)] for _ in range(KC)]
        kb.op("sp", lambda e: e.dma_start(out=self.c32, in_=self.cst[:, :]), partial=[self.dconst], dma=self.sem_c)
        kb.op("sp", lambda e: e.dma_start(out=self.sm, in_=self.small[:, :]), partial=[self.dconst], dma=self.sem_c)
        kb.op("dve", lambda e: e.tensor_copy(out=self.cb, in_=self.c32), reads=[self.dconst], partial=[self.dconst])
        self.ones_bf = self.cb[:, CS["ones"]:CS["ones"] + 128]
        base = ar.mark()
        for b in range(self.nseq):
            for c in range(KC):
                kb.op("sp", lambda e, b=b, c=c: e.dma_start(out=self.x[:, c, :], in_=self.xT[b, :, c, :]),
                      writes=self.dx[c], dma=self.sem_x[c])
            self.rope_ready = False
            for pidx, ph in enumerate(self.phases):
                kind = ph[0]
                m = ar.mark()
                if kind == "ffn":
                    nxt = self.phases[pidx + 1][0] if pidx + 1 < len(self.phases) else None
                    self.ffn(ph[1], rope_b=(b if nxt == "kv" else None))
                elif kind == "gla":
                    self.gla(ph[1])
                elif kind == "kv":
                    self.kv(b)
                elif kind == "dil":
                    self.dil(ph[1], b)
                kb.barrier()
                ar.release(m)
                self.sem_pool.extend(self.phase_sems)
                self.phase_sems = []
            for c in range(KC):
                kb.op("sp", lambda e, b=b, c=c: e.dma_start(out=self.yT[b, :, c, :], in_=self.x[:, c, :]),
                      reads=self.dx[c], dma=self.sem_out)
            kb.barrier()
            ar.release(base)

    def norm_stats(self, srcs, src_deps, n, dim, bufs, extra_scale=1.0):
        kb = self.kb
        pss, dps = self.ps("aux")
        nsrc = len(srcs)
        for i, (s_ap, s_dep) in enumerate(zip(srcs, src_deps)):
            sq, dsq = bufs["sq"].next()
            kb.op("act", lambda e, sq=sq, s_ap=s_ap: e.activation(out=sq[:, 0:n], in_=s_ap, func=AF.Square),
                  reads=s_dep, writes=[dsq])
            kb.op("pe", lambda e, sq=sq, i=i: e.matmul(pss[:, 0:n], lhsT=self.ones_bf, rhs=sq[:, 0:n],
                                                       start=(i == 0), stop=(i == nsrc - 1)),
                  reads=[dsq, self.dconst], partial=[dps], flag=True)
        lnv, dln = bufs["lnv"].next()
        kb.op("act", lambda e: e.activation(out=lnv[:, 0:n], in_=pss[:, 0:n], func=AF.Ln, scale=1.0 / dim, bias=self.epsb),
              reads=[dps, self.dconst], writes=[dln])
        kb.op("act", lambda e: e.activation(out=pss[:, 0:n], in_=lnv[:, 0:n], func=AF.Exp, scale=-0.5,
                                            bias=(self.lnsb if extra_scale != 1.0 else self.zerob)),
              reads=[dln, self.dconst], writes=[dps])
        return pss, dps

    def norm_bufs(self, n=TT):
        ar = self.ar
        return {
            "sq": Ring([(ar.alloc((n,), BF16), Dep()) for _ in range(3)]),
            "lnv": Ring([(ar.alloc((n,), F32), Dep()) for _ in range(2)]),
            "rstd": Ring([(ar.alloc((n,), F32), Dep()) for _ in range(3)]),
        }

    def small_consts(self):
        c = self.c32
        self.epsb = c[:, CS["eps"]:CS["eps"] + 1]
        self.zerob = c[:, CS["zero"]:CS["zero"] + 1]
        self.lnsb = c[:, CS["lns"]:CS["lns"] + 1]
        self.oneb = c[:, CS["one"]:CS["one"] + 1]

    def lazy_norm(self, gain_off, xn, dxn, nb):
        done = set()

        def need(tt):
            if tt not in done:
                done.add(tt)
                self.norm_x(gain_off, xn, dxn, nb, tts=[tt])
        return need

    def norm_x(self, gain_off, xn, dxn, nb, tts=range(NTT)):
        kb = self.kb
        for tt in tts:
            cs = slice(tt * TT, (tt + 1) * TT)
            rstd, drs = self.norm_stats([self.x[:, c, cs] for c in range(KC)], [[self.dx[c][tt]] for c in range(KC)],
                                        TT, D, nb)
            for c in range(KC):
                g = self.sm[:, gain_off + c:gain_off + c + 1]
                kb.op("dve", lambda e, c=c, g=g, rstd=rstd, cs=cs: e.scalar_tensor_tensor(
                    out=xn[:, c, cs], in0=self.x[:, c, cs], scalar=g, in1=rstd[:, 0:TT], op0=ALU.mult, op1=ALU.mult),
                    reads=[self.dx[c][tt], drs, self.dconst], writes=[dxn[c][tt]])

    def ffn(self, L, rope_b=None):
        kb, ar = self.kb, self.ar
        self.small_consts()
        rope_todo = self.rope_chunks(rope_b) if rope_b is not None else iter(())
        nb = self.norm_bufs()
        xn = ar.alloc((KC, S), BF16)
        dxn = [[Dep() for _ in range(NTT)] for _ in range(KC)]
        parts = [(0, 5), (5, 10), (10, 14), (14, 18), (18, 22)]
        PJ = 5
        a = ar.alloc((PJ, S), BF16)
        da = [[Dep() for _ in range(NTT)] for _ in range(PJ)]
        NWS = 3
        wslots = [(ar.alloc((2 * KC * 128,), BF16), Dep(), self.dma_sem(f"f{L}_wu{i}_{kb.nops}")) for i in range(NWS)]
        wd = [(ar.alloc((PJ, D), BF16), Dep(), self.dma_sem(f"f{L}_wd{i}_{kb.nops}")) for i in range(2)]
        hals = [(ar.alloc((2 * NTT,), F32), [Dep() for _ in range(NTT)]) for _ in range(2)]
        tg = Ring([(ar.alloc((TT,), F32), Dep()) for _ in range(4)])
        tv = Ring([(ar.alloc((TT,), F32), Dep()) for _ in range(4)])
        sg = Ring([(ar.alloc((TT,), F32), Dep()) for _ in range(2)])

        def load_wu(j):
            w, dw, sem = wslots[j % NWS]
            kb.op("pool", lambda e, w=w, j=j: e.dma_start(out=w, in_=self.din(f'wup{L}')[j, :, :], max_dma_last_dim=4096),
                  writes=[dw], dma=sem)

        def load_wd(pi):
            j0, j1 = parts[pi]
            w, dw, sem = wd[pi % 2]
            kb.op("pool", lambda e, w=w: e.dma_start(out=w[:, 0:j1 - j0, :], in_=self.din(f'wdn{L}')[:, j0:j1, :],
                                                      max_dma_last_dim=4096), writes=[dw], dma=sem)

        load_wu(0)
        load_wu(1)
        load_wd(0)
        need_norm = self.lazy_norm(SM["f_norm"] + L * 8, xn, dxn, nb)
        cpo = SM["conv"] + L * 44 * 4

        def cparam(chunk, k):
            o = cpo + chunk * 4 + k
            return self.sm[:, o:o + 1]

        steps = [(pi, j, tt) for pi, (j0, j1) in enumerate(parts) for j in range(j0, j1) for tt in range(NTT)]
        last_of_part = {}
        for idx, (pi, j, tt) in enumerate(steps):
            last_of_part[pi] = idx
        LA = 2
        down_at = {}
        for pi in range(len(parts)):
            down_at.setdefault(min(last_of_part[pi] + LA, len(steps) - 1), []).append(pi)
        pending = []
        down_done = [False] * len(parts)

        def flush(keep, maxpi=None):
            while len(pending) > keep:
                p_, fn_ = pending[0]
                if maxpi is not None and p_ > maxpi:
                    break
                if p_ > 0 and not down_done[p_ - 1]:
                    break
                pending.pop(0)
                fn_()

        def down(pi):
            j0, j1 = parts[pi]
            wdt, dwd, _ = wd[pi % 2]
            nj = j1 - j0
            for tt in range(NTT):
                cs = slice(tt * TT, (tt + 1) * TT)
                for c in range(KC):
                    pt, dp = self.ps("all")
                    for jj in range(nj):
                        kb.op("pe", lambda e: e.matmul(pt[:, :], lhsT=wdt[:, jj, c * 128:(c + 1) * 128], rhs=a[:, jj, cs],
                                                       start=(jj == 0), stop=(jj == nj - 1)),
                              reads=[dwd, da[jj][tt]], partial=[dp], flag=(jj == nj - 1))
                    kb.op("dve", lambda e: e.tensor_tensor(out=self.x[:, c, cs], in0=pt[:, :], in1=self.x[:, c, cs], op=ALU.add),
                          reads=[dp], writes=[self.dx[c][tt]])
            down_done[pi] = True
            if pi + 2 < len(parts):
                load_wd(pi + 2)

        load_wd(1)
        for idx, (pi, j, tt) in enumerate(steps):
            j0, j1 = parts[pi]
            jj = j - j0
            if tt == 0 and j + 2 < NJ:
                load_wu(j + 2)
            w, dw, _ = wslots[j % NWS]
            need_norm(tt)
            cs = slice(tt * TT, (tt + 1) * TT)
            outs = []
            for half in range(2):
                pt, dp = self.ps("all")
                for kc in range(KC):
                    kb.op("pe", lambda e: e.matmul(pt[:, :], lhsT=w[:, half * 1024 + kc * 128: half * 1024 + (kc + 1) * 128],
                                                   rhs=xn[:, kc, cs], start=(kc == 0), stop=(kc == KC - 1)),
                          reads=[dw, dxn[kc][tt]], partial=[dp], flag=(kc == KC - 1))
                chunk = j + half * NJ
                t1, dt1 = (tg if half == 0 else tv).next()
                hal, dhal = hals[half]
                kb.op("act", lambda e: e.activation(out=t1, in_=pt[:, :], func=AF.Identity, scale=cparam(chunk, 2), bias=cparam(chunk, 3)),
                      reads=[dp, self.dconst], writes=[dt1])
                if tt < NTT - 1:
                    kb.op("act", lambda e: e.activation(out=hal[:, 2 * tt:2 * tt + 2], in_=pt[:, TT - 2:TT], func=AF.Identity),
                          reads=[dp], writes=[dhal[tt]])
                outs.append((t1, dt1, pt, dp, hal, dhal, chunk))
            for k in (1, 0):
                sh = 2 - k
                for (t1, dt1, pt, dp, hal, dhal, chunk) in outs:
                    kb.op("dve", lambda e: e.scalar_tensor_tensor(out=t1[:, sh:TT], in0=pt[:, 0:TT - sh], scalar=cparam(chunk, k), in1=t1[:, sh:TT],
                                                                  op0=ALU.mult, op1=ALU.add),
                          reads=[dp, self.dconst], writes=[dt1])
            if tt > 0:
                for (t1, dt1, pt, dp, hal, dhal, chunk) in outs:
                    hp = hal[:, 2 * (tt - 1):2 * (tt - 1) + 2]
                    kb.op("dve", lambda e: e.scalar_tensor_tensor(out=t1[:, 0:1], in0=hp[:, 1:2], scalar=cparam(chunk, 1), in1=t1[:, 0:1],
                                                                  op0=ALU.mult, op1=ALU.add),
                          reads=[dhal[tt - 1], self.dconst], writes=[dt1])
                    kb.op("dve", lambda e: e.scalar_tensor_tensor(out=t1[:, 0:2], in0=hp[:, 0:2], scalar=cparam(chunk, 0), in1=t1[:, 0:2],
                                                                  op0=ALU.mult, op1=ALU.add),
                          reads=[dhal[tt - 1], self.dconst], writes=[dt1])
            (t1g, dt1g), (t1v, dt1v) = [(o_[0], o_[1]) for o_ in outs]

            def tail(t1g=t1g, dt1g=dt1g, t1v=t1v, dt1v=dt1v, jj=jj, cs=cs, tt=tt):
                sgt, dsg = sg.next()
                kb.op("act", lambda e: e.activation(out=sgt, in_=t1g, func=AF.Silu), reads=[dt1g], writes=[dsg])
                kb.op("pool", lambda e: e.tensor_tensor(out=a[:, jj, cs], in0=t1v, in1=sgt, op=ALU.mult),
                      reads=[dsg, dt1v], writes=[da[jj][tt]])
            pending.append((pi, tail))
            flush(keep=1)
            next(rope_todo, None)
            if idx % 4 == 1:
                next(rope_todo, None)
            for p_ in down_at.get(idx, []):
                flush(keep=0, maxpi=p_)
                down(p_)
                flush(keep=1)
        assert not pending and all(down_done)
        for _ in rope_todo:
            pass
        if rope_b is not None:
            self.rope_ready = True

    def gla(self, L):
        kb, ar = self.kb, self.ar
        self.small_consts()
        c32 = self.c32
        TG = 256
        NTG = S // TG
        nb = self.norm_bufs(TG)
        win_h = self.din(f"win{L}")
        awo_h = self.din(f"awo{L}")
        wg_h = self.din(f"wg{L}")
        win = ar.alloc((KC, INW), BF16)
        wout = ar.alloc((KC, D), BF16)
        wgs = ar.alloc((512,), F32)
        dwin, dwout, dwg = Dep(), Dep(), Dep()
        s_win, s_wout, s_wg = (self.dma_sem(f"g{L}_{n}_{kb.nops}") for n in ("win", "wout", "wg"))
        for kc in range(KC):
            kb.op("pool", lambda e, kc=kc: e.dma_start(out=win[:, kc, :], in_=win_h[:, kc, :], max_dma_last_dim=4096),
                  partial=[dwin], dma=s_win)
        kb.op("sp", lambda e: e.dma_start(out=wgs[0:33, :], in_=wg_h[:, :]), writes=[dwg], dma=s_wg)
        dwin_g = {"r": dwin, "qk": dwin, "v": dwin}
        for kc in range(KC):
            kb.op("pool", lambda e, kc=kc: e.dma_start(out=wout[:, kc, :], in_=awo_h[:, kc, :], max_dma_last_dim=4096),
                  partial=[dwout], dma=s_wout)
        xn = Ring([(ar.alloc((KC, TG), BF16), Dep())])
        glr = ar.alloc((TG,), F32)
        dglr = Dep()
        kb.op("pool", lambda e: e.memset(glr[0:64, :], 0.0), writes=[dglr])
        kb.op("pool", lambda e: e.memset(glr[32:33, :], 1.0), partial=[dglr])
        e1 = (ar.alloc((512,), F32), Dep())
        lsb = (ar.alloc((512,), F32), Dep())
        Eq = (ar.alloc((4, TG), F32), Dep())
        Ek = (ar.alloc((4, TG), F32), Dep())
        Es = (ar.alloc((2, 512), F32), Dep())
        qt = (ar.alloc((4, TG), BF16), Dep())
        kt = (ar.alloc((4, TG), BF16), Dep())
        ks = Ring([(ar.alloc((512,), BF16), Dep()) for _ in range(2)])
        vb = Ring([(ar.alloc((1024,), BF16), Dep()) for _ in range(2)])
        sr = (ar.alloc((KC, TG), BF16), Dep())
        og = (ar.alloc((KC, TG), BF16), Dep())
        attm = Ring([(ar.alloc((4, 128), BF16), Dep()) for _ in range(2)])
        S32 = ar.alloc((4, 256), F32)
        dS32 = [Dep() for _ in range(4)]
        Sb = Ring([(ar.alloc((4, 256), BF16), [Dep() for _ in range(4)]) for _ in range(2)])
        sqo = (ar.alloc((1024,), BF16), Dep())
        lnvo = (ar.alloc((512,), F32), Dep())
        rso = (ar.alloc((4, 128), F32), Dep())
        tmpo = Ring([(ar.alloc((2, 128), F32), Dep()) for _ in range(2)])
        go = SM["o_norm"] + L * 2
        QS = 128.0 ** -0.5
        tiles = [dict() for _ in range(NTG)]
        st = dict(first_chunk=True, cur_Sb=None)

        def stage_A(t):
            T = tiles[t]
            cs = slice(t * TG, (t + 1) * TG)
            tt = (t * TG) // TT
            rstd, drs = self.norm_stats([self.x[:, c, cs] for c in range(KC)], [[self.dx[c][tt]] for c in range(KC)], TG, D, nb)
            xnt, dxn = xn.next()
            T["xn"] = (xnt, dxn)
            for c in range(KC):
                g = self.sm[:, SM["a_norm"] + L * 8 + c:SM["a_norm"] + L * 8 + c + 1]
                kb.op("dve", lambda e: e.scalar_tensor_tensor(out=xnt[:, c, :], in0=self.x[:, c, cs], scalar=g, in1=rstd[:, 0:TG],
                                                              op0=ALU.mult, op1=ALU.mult),
                      reads=[self.dx[c][tt], drs, self.dconst], partial=[dxn])
            pg, dpg = self.ps("aux")
            for kc in range(KC):
                kb.op("pe", lambda e: e.matmul(pg[0:16, 0:TG], lhsT=win[:, kc, 2048:2064], rhs=xnt[:, kc, :], start=(kc == 0), stop=(kc == KC - 1)),
                      reads=[dwin_g["r"], dxn], partial=[dpg], flag=(kc == KC - 1))
            kb.op("act", lambda e: e.activation(out=glr[0:16, :], in_=pg[0:16, 0:TG], func=AF.Identity), reads=[dpg], partial=[dglr])
            for f in range(8):
                pr, dpr = self.ps("mm")
                for kc in range(KC):
                    kb.op("pe", lambda e: e.matmul(pr[:, 0:TG], lhsT=win[:, kc, 2064 + f * 128:2064 + (f + 1) * 128], rhs=xnt[:, kc, :],
                                                   start=(kc == 0), stop=(kc == KC - 1)),
                          reads=[dwin_g["r"], dxn], partial=[dpr], flag=(kc == KC - 1))
                kb.op("act", lambda e: e.activation(out=sr[0][:, f, :], in_=pr[:, 0:TG], func=AF.Silu), reads=[dpr], partial=[sr[1]])

        def stage_B(t):
            T = tiles[t]
            xnt, dxn = T["xn"]
            T["vb"], T["ks"] = {}, {}

            def LG(bk):
                bs = slice(bk * 128, (bk + 1) * 128)
                pl, dpl = self.ps("aux")
                kb.op("pe", lambda e: e.matmul(pl[:, :], lhsT=glr[0:33, bs], rhs=wgs[0:33, :], start=True, stop=True),
                      reads=[dglr, dwg], writes=[dpl])
                kb.op("act", lambda e: e.activation(out=e1[0], in_=pl[:, :], func=AF.Exp, scale=-1.0), reads=[dpl], writes=[e1[1]])
                kb.op("act", lambda e: e.activation(out=lsb[0], in_=e1[0], func=AF.Ln, bias=self.oneb),
                      reads=[e1[1], self.dconst], writes=[lsb[1]])

            def CS_(bk):
                bs = slice(bk * 128, (bk + 1) * 128)
                pb, dpb = self.ps("aux")
                for h in range(4):
                    kb.op("pe", lambda e: e.matmul(pb[:, h * 128:(h + 1) * 128], lhsT=lsb[0][:, h * 128:(h + 1) * 128],
                                                   rhs=c32[:, CS["tri"]:CS["tri"] + 128], start=True, stop=True),
                          reads=[lsb[1], self.dconst], partial=[dpb], flag=(h == 3))
                pbv = pb[:, :].rearrange("p (h c) -> p h c", h=4)
                kb.op("act", lambda e: e.activation(out=Eq[0][:, :, bs], in_=pbv, func=AF.Exp), reads=[dpb], partial=[Eq[1]])
                kb.op("act", lambda e: e.activation(out=Ek[0][:, :, bs], in_=pbv, func=AF.Exp, scale=-1.0), reads=[dpb], partial=[Ek[1]])
                pu, dpu = self.ps("aux")
                kb.op("pe", lambda e: e.matmul(pu[:, :], lhsT=c32[:, CS["ut"]:CS["ut"] + 128], rhs=lsb[0], start=True, stop=True),
                      reads=[lsb[1], self.dconst], writes=[dpu])
                kb.op("act", lambda e: e.activation(out=Es[0][:, bk, :], in_=pu[:, :], func=AF.Exp), reads=[dpu], partial=[Es[1]])

            def VT(bk):
                bs = slice(bk * 128, (bk + 1) * 128)
                vbt, dvb = vb.next()
                T["vb"][bk] = (vbt, dvb)
                for i in range(2):
                    pv, dpv = self.ps("mm")
                    for kc in range(KC):
                        kb.op("pe", lambda e: e.matmul(pv[:, :], lhsT=xnt[:, kc, bs], rhs=win[:, kc, 1024 + i * 512:1024 + (i + 1) * 512],
                                                       start=(kc == 0), stop=(kc == KC - 1)),
                              reads=[dwin_g["v"], dxn], partial=[dpv], flag=(kc == KC - 1))
                    kb.op("act", lambda e: e.activation(out=vbt[:, i * 512:(i + 1) * 512], in_=pv[:, :], func=AF.Identity),
                          reads=[dpv], partial=[dvb])

            def KT(bk):
                bs = slice(bk * 128, (bk + 1) * 128)
                pk, dpk = self.ps("mm")
                for kc in range(KC):
                    kb.op("pe", lambda e: e.matmul(pk[:, :], lhsT=xnt[:, kc, bs], rhs=win[:, kc, 512:1024], start=(kc == 0), stop=(kc == KC - 1)),
                          reads=[dwin_g["qk"], dxn], partial=[dpk], flag=(kc == KC - 1))
                kst, dks = ks.next()
                T["ks"][bk] = (kst, dks)
                kb.op("dve", lambda e: e.tensor_tensor(out=kst, in0=pk[:, :], in1=Es[0][:, bk, :], op=ALU.mult),
                      reads=[dpk, Es[1]], writes=[dks])

            LG(0)
            VT(0)
            CS_(0)
            LG(1)
            VT(1)
            CS_(1)
            KT(0)
            KT(1)
            for oc in range(8):
                pq, dpq = self.ps("mm")
                for kc in range(KC):
                    kb.op("pe", lambda e: e.matmul(pq[:, 0:TG], lhsT=win[:, kc, oc * 128:(oc + 1) * 128], rhs=xnt[:, kc, :],
                                                   start=(kc == 0), stop=(kc == KC - 1)),
                          reads=[dwin_g["qk"], dxn], partial=[dpq], flag=(kc == KC - 1))
                if oc < 4:
                    kb.op("dve", lambda e: e.scalar_tensor_tensor(out=qt[0][:, oc, :], in0=pq[:, 0:TG], scalar=QS, in1=Eq[0][:, oc, :],
                                                                  op0=ALU.mult, op1=ALU.mult), reads=[dpq, Eq[1]], partial=[qt[1]])
                else:
                    kb.op("dve", lambda e: e.tensor_tensor(out=kt[0][:, oc - 4, :], in0=pq[:, 0:TG], in1=Ek[0][:, oc - 4, :], op=ALU.mult),
                          reads=[dpq, Ek[1]], partial=[kt[1]])

        def stage_C(t):
            T = tiles[t]
            for bk in range(2):
                bs = slice(bk * 128, (bk + 1) * 128)
                kst, dks = T["ks"][bk]
                vbt, dvb = T["vb"][bk]
                first_chunk = st["first_chunk"]
                pa, dpa = self.ps("aux")
                for h in range(4):
                    kb.op("pe", lambda e: e.matmul(pa[:, h * 128:(h + 1) * 128], lhsT=kt[0][:, h, bs], rhs=qt[0][:, h, bs], start=True, stop=True),
                          reads=[kt[1], qt[1]], partial=[dpa], flag=(h == 3))
                am, dam = attm.next()
                kb.op("dve", lambda e: e.tensor_tensor(
                    out=am, in0=pa[:, :].rearrange("p (h c) -> p h c", h=4),
                    in1=c32[:, CS["bd4"]:CS["bd4"] + 512].rearrange("p (h c) -> p h c", h=4), op=ALU.mult),
                    reads=[dpa, self.dconst], writes=[dam])
                qcols = slice(bk * 128, (bk + 1) * 128)
                po2 = [self.ps("aux"), self.ps("aux")]
                for h in range(4):
                    po, dpo = po2[h // 2]
                    for vc in range(2):
                        oc_ = slice(((h % 2) * 2 + vc) * 128, ((h % 2) * 2 + vc + 1) * 128)
                        last = (h % 2 == 1 and vc == 1)
                        kb.op("pe", lambda e: e.matmul(po[:, oc_], lhsT=vbt[:, h * 256 + vc * 128:h * 256 + (vc + 1) * 128], rhs=am[:, h, :],
                                                       start=True, stop=first_chunk), reads=[dvb, dam], partial=[dpo], flag=(last and first_chunk))
                        if not first_chunk:
                            sbt, dsb = st["cur_Sb"]
                            kb.op("pe", lambda e: e.matmul(po[:, oc_], lhsT=sbt[:, h, vc * 128:(vc + 1) * 128], rhs=qt[0][:, h, qcols],
                                                           start=False, stop=True), reads=[dsb[h], qt[1]], partial=[dpo], flag=last)
                nsb, dnsb = Sb.next()
                for h in range(4):
                    pst, dpst = self.ps("mm")
                    kb.op("pe", lambda e: e.matmul(pst[:, 0:256], lhsT=kst[:, h * 128:(h + 1) * 128], rhs=vbt[:, h * 256:(h + 1) * 256],
                                                   start=True, stop=True), reads=[dks, dvb], writes=[dpst])
                    if first_chunk:
                        kb.op("dve", lambda e: e.tensor_copy(out=S32[:, h, :], in_=pst[:, 0:256]), reads=[dpst], writes=[dS32[h]])
                    else:
                        dcol = bk * 128 + 127
                        kb.op("dve", lambda e: e.scalar_tensor_tensor(
                            out=S32[:, h, :], in0=S32[:, h, :], scalar=Eq[0][:, h, dcol:dcol + 1], in1=pst[:, 0:256],
                            op0=ALU.mult, op1=ALU.add), reads=[dpst, Eq[1]], writes=[dS32[h]])
                    kb.op("pool", lambda e: e.tensor_copy(out=nsb[:, h, :], in_=S32[:, h, :]), reads=[dS32[h]], writes=[dnsb[h]])
                st["cur_Sb"] = (nsb, dnsb)
                st["first_chunk"] = False
                pss, dpss = self.ps("aux")
                for i in range(2):
                    po, dpo = po2[i]
                    kb.op("act", lambda e: e.activation(out=sqo[0][:, i * 512:(i + 1) * 512], in_=po[:, :], func=AF.Square),
                          reads=[dpo], partial=[sqo[1]])
                for h in range(4):
                    for vc in range(2):
                        kb.op("pe", lambda e: e.matmul(pss[:, h * 128:(h + 1) * 128], lhsT=self.ones_bf,
                                                       rhs=sqo[0][:, (h * 2 + vc) * 128:(h * 2 + vc + 1) * 128],
                                                       start=(vc == 0), stop=(vc == 1)),
                              reads=[sqo[1], self.dconst], partial=[dpss], flag=(h == 3 and vc == 1))
                kb.op("act", lambda e: e.activation(out=lnvo[0], in_=pss[:, :], func=AF.Ln, scale=1.0 / 256.0, bias=self.epsb),
                      reads=[dpss, self.dconst], writes=[lnvo[1]])
                kb.op("act", lambda e: e.activation(out=rso[0].rearrange("p h c -> p (h c)"), in_=lnvo[0], func=AF.Exp, scale=-0.5),
                      reads=[lnvo[1]], writes=[rso[1]])
                ogv = og[0].rearrange("p (h v) c -> p h v c", v=2)
                srv = sr[0].rearrange("p (h v) c -> p h v c", v=2)
                for i in range(2):
                    po, dpo = po2[i]
                    pov = po[:, :].rearrange("p (h v c) -> p h v c", h=2, v=2)
                    for vc in range(2):
                        tm, dtm = tmpo.next()
                        g = self.sm[:, go + vc:go + vc + 1]
                        kb.op("dve", lambda e: e.scalar_tensor_tensor(
                            out=tm, in0=pov[:, :, vc, :], scalar=g, in1=rso[0][:, 2 * i:2 * i + 2, :], op0=ALU.mult, op1=ALU.mult),
                            reads=[dpo, rso[1], self.dconst], writes=[dtm])
                        kb.op("pool", lambda e: e.tensor_tensor(
                            out=ogv[:, 2 * i:2 * i + 2, vc, qcols], in0=tm, in1=srv[:, 2 * i:2 * i + 2, vc, qcols], op=ALU.mult),
                            reads=[dtm, sr[1]], partial=[og[1]])

        def stage_W(t):
            cs = slice(t * TG, (t + 1) * TG)
            tt = (t * TG) // TT
            for c in range(KC):
                pw, dpw = self.ps("mm")
                for f in range(KC):
                    kb.op("pe", lambda e: e.matmul(pw[:, 0:TG], lhsT=wout[:, f, c * 128:(c + 1) * 128], rhs=og[0][:, f, :],
                                                   start=(f == 0), stop=(f == KC - 1)),
                          reads=[dwout, og[1]], partial=[dpw], flag=(f == KC - 1))
                kb.op("dve", lambda e: e.tensor_tensor(out=self.x[:, c, cs], in0=pw[:, 0:TG], in1=self.x[:, c, cs], op=ALU.add),
                      reads=[dpw], writes=[self.dx[c][tt]])

        stage_A(0)
        for t in range(NTG):
            stage_B(t)
            stage_C(t)
            if t + 1 < NTG:
                stage_A(t + 1)
            stage_W(t)


    def rope_compute(self, b):
        kb, ar = self.kb, self.ar
        c32 = self.c32
        m = ar.mark()
        cos2 = ar.alloc((S,), F32)
        sin2 = ar.alloc((S,), F32)
        drope = Dep()
        posi = ar.alloc((S,), I32) if False else ar.alloc((S,), F32).bitcast(I32)
        y = ar.alloc((S,), F32)
        r_ = ar.alloc((S,), F32)
        t_ = ar.alloc((S,), F32)
        dp_, dy, dr, dt_ = Dep(), Dep(), Dep(), Dep()
        sem = self.dma_sem(f"pos_{kb.nops}")
        R = slice(0, 32)
        invf = c32[R, CS["invf"]:CS["invf"] + 1]
        sgn = c32[R, CS["sgn"]:CS["sgn"] + 1]
        kb.op("sp", lambda e: e.dma_start(out=posi[R, :], in_=self.din("posb")[b, :, :]), writes=[dp_], dma=sem)
        kb.op("dve", lambda e: e.tensor_copy(out=t_[R, :], in_=posi[R, :]), reads=[dp_], writes=[dt_])
        kb.op("dve", lambda e: e.tensor_scalar(out=y[R, :], in0=t_[R, :], scalar1=invf, scalar2=1.0 / (2.0 * math.pi),
                                               op0=ALU.mult, op1=ALU.mult), reads=[dt_, self.dconst], writes=[dy])
        for which, dst in (("sin", sin2), ("cos", cos2)):
            if which == "cos":
                kb.op("dve", lambda e: e.tensor_scalar(out=y[R, :], in0=y[R, :], scalar1=0.25, scalar2=None, op0=ALU.add),
                      reads=[dy], writes=[dy])
            yi = r_.bitcast(I32)
            kb.op("dve", lambda e: e.tensor_copy(out=yi[R, :], in_=y[R, :]), reads=[dy], writes=[dr])
            kb.op("dve", lambda e: e.tensor_copy(out=t_[R, :], in_=yi[R, :]), reads=[dr], writes=[dt_])
            kb.op("dve", lambda e: e.tensor_tensor(out=r_[R, :], in0=y[R, :], in1=t_[R, :], op=ALU.subtract),
                  reads=[dy, dt_], writes=[dr])
            kb.op("dve", lambda e: e.tensor_scalar(out=t_[R, :], in0=r_[R, :], scalar1=0.5, scalar2=None, op0=ALU.is_gt),
                  reads=[dr], writes=[dt_])
            kb.op("dve", lambda e: e.tensor_tensor(out=r_[R, :], in0=r_[R, :], in1=t_[R, :], op=ALU.subtract),
                  reads=[dt_], writes=[dr])
            kb.op("dve", lambda e: e.tensor_scalar(out=t_[R, :], in0=r_[R, :], scalar1=-0.5, scalar2=None, op0=ALU.is_lt),
                  reads=[dr], writes=[dt_])
            kb.op("dve", lambda e: e.tensor_tensor(out=r_[R, :], in0=r_[R, :], in1=t_[R, :], op=ALU.add),
                  reads=[dt_], writes=[dr])
            kb.op("act", lambda e, dst=dst: e.activation(out=dst[R, :], in_=r_[R, :], func=AF.Sin, scale=6.283185),
                  reads=[dr], partial=[drope])
        kb.op("dve", lambda e: e.tensor_scalar(out=sin2[R, :], in0=sin2[R, :], scalar1=sgn, scalar2=None, op0=ALU.mult),
              reads=[drope, self.dconst], writes=[drope])
        rs = self.scratch("rope", (2, 32, S), F32)
        sem2 = self.dma_sem("rope_out")
        kb.op("sp", lambda e: e.dma_start(out=rs[0, :, :], in_=cos2[R, :]), reads=[drope], dma=sem2)
        kb.op("sp", lambda e: e.dma_start(out=rs[1, :, :], in_=sin2[R, :]), reads=[drope], dma=sem2)
        kb.barrier()
        ar.release(m)

    def rope_chunks(self, b):
        kb, ar = self.kb, self.ar
        c32 = self.c32
        rs = self.scratch("rope", (2, 32, S), F32)
        R = slice(0, 32)
        invf = c32[R, CS["invf"]:CS["invf"] + 1]
        sgn = c32[R, CS["sgn"]:CS["sgn"] + 1]
        posi = ar.alloc((TT,), F32).bitcast(I32)
        y = ar.alloc((TT,), F32)
        r_ = ar.alloc((TT,), F32)
        t_ = ar.alloc((TT,), F32)
        outs = {"sin": ar.alloc((TT,), F32), "cos": ar.alloc((TT,), F32)}
        dp_, dy, dr, dt_, dout = Dep(), Dep(), Dep(), Dep(), {"sin": Dep(), "cos": Dep()}
        sem_i, sem_o = self.dma_sem("ropec_in"), self.dma_sem("ropec_out")

        def chunk(k4):
            cs = slice(k4 * TT, (k4 + 1) * TT)
            kb.op("sp", lambda e: e.dma_start(out=posi[R, :], in_=self.din("posb")[b, :, cs]), writes=[dp_], dma=sem_i)
            yield
            kb.op("pool", lambda e: e.tensor_copy(out=t_[R, :], in_=posi[R, :]), reads=[dp_], writes=[dt_])
            yield
            kb.op("pool", lambda e: e.tensor_scalar(out=y[R, :], in0=t_[R, :], scalar1=invf, scalar2=1.0 / (2.0 * math.pi),
                                                    op0=ALU.mult, op1=ALU.mult), reads=[dt_, self.dconst], writes=[dy])
            yield
            for which in ("sin", "cos"):
                dst, dd = outs[which], dout[which]
                if which == "cos":
                    kb.op("pool", lambda e: e.tensor_scalar(out=y[R, :], in0=y[R, :], scalar1=0.25, scalar2=None, op0=ALU.add),
                          reads=[dy], writes=[dy])
                    yield
                yi = r_.bitcast(I32)
                kb.op("pool", lambda e: e.tensor_copy(out=yi[R, :], in_=y[R, :]), reads=[dy], writes=[dr])
                yield
                kb.op("pool", lambda e: e.tensor_copy(out=t_[R, :], in_=yi[R, :]), reads=[dr], writes=[dt_])
                yield
                kb.op("pool", lambda e: e.tensor_tensor(out=r_[R, :], in0=y[R, :], in1=t_[R, :], op=ALU.subtract), reads=[dy, dt_], writes=[dr])
                yield
                kb.op("pool", lambda e: e.tensor_scalar(out=t_[R, :], in0=r_[R, :], scalar1=0.5, scalar2=None, op0=ALU.is_gt),
                      reads=[dr], writes=[dt_])
                yield
                kb.op("pool", lambda e: e.tensor_tensor(out=r_[R, :], in0=r_[R, :], in1=t_[R, :], op=ALU.subtract), reads=[dt_], writes=[dr])
                yield
                kb.op("pool", lambda e: e.tensor_scalar(out=t_[R, :], in0=r_[R, :], scalar1=-0.5, scalar2=None, op0=ALU.is_lt),
                      reads=[dr], writes=[dt_])
                yield
                kb.op("pool", lambda e: e.tensor_tensor(out=r_[R, :], in0=r_[R, :], in1=t_[R, :], op=ALU.add), reads=[dt_], writes=[dr])
                yield
                kb.op("act", lambda e: e.activation(out=dst[R, :], in_=r_[R, :], func=AF.Sin, scale=6.283185), reads=[dr], writes=[dd])
                yield
            kb.op("pool", lambda e: e.tensor_scalar(out=outs["sin"][R, :], in0=outs["sin"][R, :], scalar1=sgn, scalar2=None, op0=ALU.mult),
                  reads=[self.dconst], writes=[dout["sin"]])
            yield
            kb.op("sp", lambda e: e.dma_start(out=rs[0, :, cs], in_=outs["cos"][R, :]), reads=[dout["cos"]], dma=sem_o)
            yield
            kb.op("sp", lambda e: e.dma_start(out=rs[1, :, cs], in_=outs["sin"][R, :]), reads=[dout["sin"]], dma=sem_o)
            yield

        def gen():
            for k4 in range(NTT):
                yield from chunk(k4)
        return gen()

    def rope_tables(self, b):
        kb, ar = self.kb, self.ar
        rs = self.scratch("rope", (2, 32, S), F32)
        cos2 = ar.alloc((S,), F32)
        sin2 = ar.alloc((S,), F32)
        drope = Dep()
        sem = self.dma_sem("rope_in")
        R = slice(0, 32)
        kb.op("sp", lambda e: e.dma_start(out=cos2[R, :], in_=rs[0, :, :]), partial=[drope], dma=sem)
        kb.op("sp", lambda e: e.dma_start(out=sin2[R, :], in_=rs[1, :, :]), partial=[drope], dma=sem)
        self.cos2, self.sin2, self.drope = cos2, sin2, drope


    def head_bufs(self):
        ar = self.ar
        return {
            "nb": self.norm_bufs(TT),
            "kn": Ring([(ar.alloc((TT,), BF16), Dep()) for _ in range(4)]),
            "sq2": Ring([(ar.alloc((TT,), BF16), Dep()) for _ in range(3)]),
            "t1": Ring([(ar.alloc((TT,), F32), Dep()) for _ in range(2)]),
            "t2": Ring([(ar.alloc((TT,), F32), Dep()) for _ in range(3)]),
        }

    def head_stage0b(self, it, hb):
        kb = self.kb
        sq, dsq = hb["sq2"].next()
        kb.op("act", lambda e: e.activation(out=sq, in_=it["pt"][:, :], func=AF.Square), reads=[it["dp"]], writes=[dsq])
        it["sq"], it["dsq"] = sq, dsq

    def head_stage1(self, it, hb):
        kb = self.kb
        pt, dp = it["pt"], it["dp"]
        pss, dps = self.ps("aux")
        kb.op("pe", lambda e: e.matmul(pss[:, :], lhsT=self.ones_bf, rhs=it["sq"], start=True, stop=True),
              reads=[it["dsq"], self.dconst], writes=[dps])
        lnv, dln = hb["nb"]["lnv"].next()
        rstd, drs = hb["nb"]["rstd"].next()
        kb.op("act", lambda e: e.activation(out=lnv, in_=pss[:, :], func=AF.Ln, scale=1.0 / 128.0, bias=self.epsb),
              reads=[dps, self.dconst], writes=[dln])
        kb.op("act", lambda e: e.activation(out=rstd, in_=lnv, func=AF.Exp, scale=-0.5,
                                            bias=(self.lnsb if it["extra"] != 1.0 else self.zerob)),
              reads=[dln, self.dconst], writes=[drs])
        kn, dkn = hb["kn"].next()
        kb.op("dve", lambda e: e.scalar_tensor_tensor(out=kn, in0=pt[:, :], scalar=it["gcol"], in1=rstd[:, 0:TT], op0=ALU.mult, op1=ALU.mult),
              reads=[dp, drs, self.dconst], writes=[dkn])
        it["kn"], it["dkn"] = kn, dkn

    def head_stage2(self, it, hb):
        kb = self.kb
        kn, dkn, tt, d, dst, ddst = it["kn"], it["dkn"], it["tt"], it["d"], it["dst"], it["ddst"]
        cs = slice(tt * TT, (tt + 1) * TT)
        R = slice(0, 32)
        psw, dpsw = self.ps("aux")
        kb.op("pe", lambda e: e.matmul(psw[:, :], lhsT=self.cb[:, CS["perm"]:CS["perm"] + 128], rhs=kn, start=True, stop=True),
              reads=[dkn, self.dconst], writes=[dpsw])
        t1, dt1 = hb["t1"].next()
        t2, dt2 = hb["t2"].next()
        kb.op("pool", lambda e: e.tensor_tensor(out=t1[R, :], in0=kn[R, :], in1=self.cos2[R, cs], op=ALU.mult),
              reads=[dkn, self.drope], writes=[dt1])
        kb.op("dve", lambda e: e.tensor_tensor(out=t2[R, :], in0=psw[R, :], in1=self.sin2[R, cs], op=ALU.mult),
              reads=[dpsw, self.drope], writes=[dt2])
        nj = TT // d
        j0 = tt * nj
        if d == 1:
            dv_all = dst[:, cs]
            dv_rot = dst[R, cs]
            src_all, s1, s2 = kn, t1[R, :], t2[R, :]
        else:
            dvw = dst.rearrange("p (r j) -> p r j", r=d)
            dv_all = dvw[:, :, j0:j0 + nj]
            dv_rot = dvw[R, :, j0:j0 + nj]
            src_all = kn.rearrange("p (j r) -> p r j", r=d)
            s1 = t1[R, :].rearrange("p (j r) -> p r j", r=d)
            s2 = t2[R, :].rearrange("p (j r) -> p r j", r=d)
        kb.op("act", lambda e: e.activation(out=dv_all, in_=src_all, func=AF.Identity), reads=[dkn], writes=[ddst[tt]])
        kb.op("dve", lambda e: e.tensor_tensor(out=dv_rot, in0=s1, in1=s2, op=ALU.add), reads=[dt1, dt2], writes=[ddst[tt]])
        if it.get("done") is not None:
            it["done"]()

    def head_pipeline(self, items, stage0, hb):
        n = len(items)
        for step in range(n + 2):
            if step < n:
                stage0(items[step])
                self.head_stage0b(items[step], hb)
            if 0 <= step - 1 < n:
                self.head_stage1(items[step - 1], hb)
            if 0 <= step - 2 < n:
                self.head_stage2(items[step - 2], hb)

    def kv(self, b):
        kb, ar = self.kb, self.ar
        self.small_consts()
        Kc = self.scratch("Kc", (24, 128, S), BF16)
        Vc = self.scratch("Vc", (3, 128, 16, 1024), BF16)
        if not self.rope_ready:
            self.rope_compute(b)
        self.rope_tables(b)
        hb = self.head_bufs()
        kvn = ar.alloc((KC, S), BF16)
        dkvn = [[Dep() for _ in range(NTT)] for _ in range(KC)]
        wks = [(ar.alloc((KC, 128), BF16), Dep(), self.dma_sem(f"wk{i}_{kb.nops}")) for i in range(3)]
        wvs2 = [(ar.alloc((KC, 1024), BF16), Dep(), self.dma_sem(f"wv{i}_{kb.nops}")) for i in range(2)]
        kbf = [(ar.alloc((S,), BF16), [Dep() for _ in range(NTT)], self.dma_sem(f"kbf{i}_{kb.nops}")) for i in range(2)]
        vts = [(ar.alloc((1024,), BF16), Dep(), self.dma_sem(f"vt{i}_{kb.nops}")) for i in range(2)]
        wk_h, wv_h = self.din("wk"), self.din("wv")

        def load_wk(gh):
            w, dw, sem = wks[gh % 3]
            kb.op("pool", lambda e: e.dma_start(out=w, in_=wk_h[gh, :, :, :], max_dma_last_dim=4096), writes=[dw], dma=sem)

        def load_wv(g):
            wv_, dwv, semv = wvs2[g % 2]
            for kc in range(KC):
                kb.op("pool", lambda e: e.dma_start(out=wv_[:, kc, :], in_=wv_h[g, :, kc, :], max_dma_last_dim=4096),
                      writes=[dwv] if kc == 0 else [], partial=[] if kc == 0 else [dwv], dma=semv)

        load_wk(0)
        load_wk(1)
        load_wv(0)
        load_wv(1)
        need_norm = self.lazy_norm(SM["kv_norm"], kvn, dkvn, hb["nb"])
        items = []
        for gh in range(24):
            g = gh // 8
            dst, ddst, sem = kbf[gh % 2]
            for tt in range(NTT):
                it = dict(gh=gh, tt=tt, d=DIL[g], dst=dst, ddst=ddst, extra=1.0,
                          gcol=self.sm[:, SM["k_norm"] + g:SM["k_norm"] + g + 1], done=None)
                if tt == NTT - 1:
                    it["done"] = (lambda gh=gh, dst=dst, ddst=ddst, sem=sem: kb.op(
                        "sp", lambda e: e.dma_start(out=Kc[gh, :, :], in_=dst), reads=ddst, dma=sem))
                items.append(it)

        def stage0(it):
            gh, tt = it["gh"], it["tt"]
            if tt == 0 and gh + 2 < 24:
                load_wk(gh + 2)
            w, dw, _ = wks[gh % 3]
            need_norm(tt)
            cs = slice(tt * TT, (tt + 1) * TT)
            pt, dp = self.ps("mm")
            for kc in range(KC):
                kb.op("pe", lambda e: e.matmul(pt[:, :], lhsT=w[:, kc, :], rhs=kvn[:, kc, cs], start=(kc == 0), stop=(kc == KC - 1)),
                      reads=[dw, dkvn[kc][tt]], partial=[dp], flag=(kc == KC - 1))
            it["pt"], it["dp"] = pt, dp

        self.head_pipeline(items, stage0, hb)
        for g in range(3):
            d = DIL[g]
            nbk = (S // d) // 128
            if g == 1:
                load_wv(2)
            wv_, dwv, semv = wvs2[g % 2]
            for blk in range(16):
                r, n = blk // nbk, blk % nbk
                t0 = r + n * 128 * d
                tsl = slice(t0, t0 + 127 * d + 1, d)
                tts = sorted(set([t0 // TT, (t0 + 127 * d) // TT]))
                vt, dvt, semt = vts[blk % 2]
                for i in range(2):
                    pv, dpv = self.ps("mm")
                    for kc in range(KC):
                        kb.op("pe", lambda e: e.matmul(pv[:, :], lhsT=kvn[:, kc, tsl], rhs=wv_[:, kc, i * 512:(i + 1) * 512],
                                                       start=(kc == 0), stop=(kc == KC - 1)),
                              reads=[dwv] + [dkvn[kc][t_] for t_ in tts], partial=[dpv], flag=(kc == KC - 1))
                    if i == 0:
                        kb.op("act", lambda e: e.activation(out=vt[:, 0:512], in_=pv[:, :], func=AF.Identity), reads=[dpv], writes=[dvt])
                    else:
                        kb.op("dve", lambda e: e.tensor_copy(out=vt[:, 512:1024], in_=pv[:, :]), reads=[dpv], partial=[dvt])
                kb.op("sp", lambda e: e.dma_start(out=Vc[g, :, blk, :], in_=vt), reads=[dvt], dma=semt)

    def dil(self, J, b):
        kb, ar = self.kb, self.ar
        self.small_consts()
        Kc = self.scratch("Kc", (24, 128, S), BF16)
        Vc = self.scratch("Vc", (3, 128, 16, 1024), BF16)
        Qc = self.scratch("Qc", (24, 128, S), BF16)
        wq_h, wo_h = self.din(f"wq{J}"), self.din(f"bwo{J}")
        m_q = ar.mark()
        self.rope_tables(b)
        hb = self.head_bufs()
        xn = ar.alloc((KC, S), BF16)
        dxn = [[Dep() for _ in range(NTT)] for _ in range(KC)]
        wqs = [(ar.alloc((KC, 128), BF16), Dep(), self.dma_sem(f"wq{i}_{kb.nops}")) for i in range(3)]
        qbf = [(ar.alloc((S,), BF16), [Dep() for _ in range(NTT)], self.dma_sem(f"qbf{i}_{kb.nops}")) for i in range(2)]

        def load_wq(gh):
            w, dw, sem = wqs[gh % 3]
            kb.op("pool", lambda e: e.dma_start(out=w, in_=wq_h[gh, :, :, :], max_dma_last_dim=4096), writes=[dw], dma=sem)

        load_wq(0)
        load_wq(1)
        need_norm = self.lazy_norm(SM["b_norm"] + J * 8, xn, dxn, hb["nb"])
        items = []
        for gh in range(24):
            g = gh // 8
            dst, ddst, sem = qbf[gh % 2]
            for tt in range(NTT):
                it = dict(gh=gh, tt=tt, d=DIL[g], dst=dst, ddst=ddst, extra=128.0 ** -0.5,
                          gcol=self.sm[:, SM["q_norm"] + J * 3 + g:SM["q_norm"] + J * 3 + g + 1], done=None)
                if tt == NTT - 1:
                    it["done"] = (lambda gh=gh, dst=dst, ddst=ddst, sem=sem: kb.op(
                        "sp", lambda e: e.dma_start(out=Qc[gh, :, :], in_=dst), reads=ddst, dma=sem))
                items.append(it)

        def stage0(it):
            gh, tt = it["gh"], it["tt"]
            if tt == 0 and gh + 2 < 24:
                load_wq(gh + 2)
            w, dw, _ = wqs[gh % 3]
            need_norm(tt)
            cs = slice(tt * TT, (tt + 1) * TT)
            pt, dp = self.ps("mm")
            for kc in range(KC):
                kb.op("pe", lambda e: e.matmul(pt[:, :], lhsT=w[:, kc, :], rhs=xn[:, kc, cs], start=(kc == 0), stop=(kc == KC - 1)),
                      reads=[dw, dxn[kc][tt]], partial=[dp], flag=(kc == KC - 1))
            it["pt"], it["dp"] = pt, dp

        self.head_pipeline(items, stage0, hb)
        kb.barrier()
        ar.release(m_q)
        ob = ar.alloc((KC, S), BF16)
        dob = [Dep() for _ in range(KC)]
        NDs = [(ar.alloc((2, S), F32), Dep()) for _ in range(2)]
        NSL = 3
        slots = [dict(K=ar.alloc((S,), BF16), Q=ar.alloc((S,), BF16), V=ar.alloc((16, 128), BF16), d=Dep(),
                      sem=self.dma_sem(f"att{i}_{kb.nops}")) for i in range(NSL)]
        pts = Ring([(ar.alloc((256,), BF16), Dep()) for _ in range(8)])
        wo = ar.alloc((KC, D), BF16)
        dwo = Dep()
        semwo = self.dma_sem(f"bwo_{kb.nops}")
        for kc in range(KC):
            kb.op("pool", lambda e: e.dma_start(out=wo[:, kc, :], in_=wo_h[:, kc, :], max_dma_last_dim=4096), partial=[dwo], dma=semwo)
        m2 = self.cb[:, CS["m2"]:CS["m2"] + 256]
        ident = self.cb[:, CS["ident"]:CS["ident"] + 128]
        combos = [(h, g) for h in range(8) for g in range(3)]
        deferred = []

        def load(ci):
            h, g = combos[ci]
            gh = g * 8 + h
            sl = slots[ci % NSL]
            kb.op("sp", lambda e: e.dma_start(out=sl["K"], in_=Kc[gh, :, :]), writes=[sl["d"]], dma=sl["sem"])
            kb.op("sp", lambda e: e.dma_start(out=sl["Q"], in_=Qc[gh, :, :]), partial=[sl["d"]], dma=sl["sem"])
            kb.op("sp", lambda e: e.dma_start(out=sl["V"], in_=Vc[g, :, :, h * 128:(h + 1) * 128]), partial=[sl["d"]], dma=sl["sem"])

        load(0)
        load(1)
        for ci, (h, g) in enumerate(combos):
            if ci + 2 < len(combos):
                load(ci + 2)
            sl = slots[ci % NSL]
            Kt, Qt, Vt, dsl = sl["K"], sl["Q"], sl["V"], sl["d"]
            ND, dND = NDs[h % 2]
            d = DIL[g]
            nbk = (S // d) // 128
            blocks = []
            for r in range(d):
                for n in range(nbk):
                    blocks.append(dict(r=r, n=n, blk=r * nbk + n, N=(256 if n < nbk - 1 else 128)))

            def att_s0(bi):
                bd = blocks[bi]
                c0, N = bd["blk"] * 128, bd["N"]
                psc, dpsc = self.ps("mm")
                kb.op("pe", lambda e: e.matmul(psc[:, 0:N], lhsT=Kt[:, c0:c0 + 128], rhs=Qt[:, c0:c0 + N], start=True, stop=False),
                      reads=[dsl], partial=[dpsc], flag=False)
                kb.op("pe", lambda e: e.matmul(psc[:, 0:N], lhsT=ident, rhs=m2[:, 0:N], start=False, stop=True),
                      reads=[self.dconst], partial=[dpsc], flag=True)
                pT, dpT = pts.next()
                kb.op("act", lambda e: e.activation(out=pT[:, 0:N], in_=psc[:, 0:N], func=AF.Exp), reads=[dpsc], writes=[dpT])
                bd["pT"], bd["dpT"] = pT, dpT

            def att_s1(bi):
                bd = blocks[bi]
                r, n, blk = bd["r"], bd["n"], bd["blk"]
                pT, dpT = bd["pT"], bd["dpT"]
                prev = (blocks[bi - 1]["pT"], blocks[bi - 1]["dpT"]) if n > 0 else None
                po, dpo = self.ps("aux")
                rd = [dsl, dpT] + ([prev[1]] if prev is not None else [])
                kb.op("pe", lambda e: e.matmul(po[:, 0:128], lhsT=Vt[:, blk, :], rhs=pT[:, 0:128], start=True, stop=(prev is None)),
                      reads=rd, partial=[dpo], flag=False)
                if prev is not None:
                    kb.op("pe", lambda e: e.matmul(po[:, 0:128], lhsT=Vt[:, blk - 1, :], rhs=prev[0][:, 128:256], start=False, stop=True),
                          reads=rd, partial=[dpo], flag=False)
                kb.op("pe", lambda e: e.matmul(po[:, 128:256], lhsT=self.ones_bf, rhs=pT[:, 0:128], start=True, stop=(prev is None)),
                      reads=rd + [self.dconst], partial=[dpo], flag=(prev is None))
                if prev is not None:
                    kb.op("pe", lambda e: e.matmul(po[:, 128:256], lhsT=self.ones_bf, rhs=prev[0][:, 128:256], start=False, stop=True),
                          reads=rd + [self.dconst], partial=[dpo], flag=True)
                t0 = r + n * 128 * d
                ndv = ND[:, :, t0:t0 + 127 * d + 1:d]
                pov = po[:, 0:256].rearrange("p (a c) -> p a c", a=2)
                if g == 0:
                    kb.op("dve", lambda e: e.tensor_copy(out=ndv, in_=pov), reads=[dpo], writes=[dND])
                else:
                    kb.op("dve", lambda e: e.tensor_tensor(out=ndv, in0=pov, in1=ndv, op=ALU.add), reads=[dpo], writes=[dND])

            SK = 3
            for step in range(len(blocks) + SK):
                if step < len(blocks):
                    att_s0(step)
                if step - SK >= 0:
                    att_s1(step - SK)
                if deferred and step % 2 == 1:
                    deferred.pop(0)()
            if g == 2:
                for k4 in range(NTT):
                    def fin(ND=ND, dND=dND, h=h, k4=k4):
                        fc = slice(k4 * TT, (k4 + 1) * TT)
                        kb.op("act", lambda e: e.activation(out=ND[:, 1, fc], in_=ND[:, 1, fc], func=AF.Ln), reads=[], writes=[dND])
                        kb.op("act", lambda e: e.activation(out=ND[:, 1, fc], in_=ND[:, 1, fc], func=AF.Exp, scale=-1.0), reads=[], writes=[dND])
                        kb.op("dve", lambda e: e.tensor_tensor(out=ob[:, h, fc], in0=ND[:, 0, fc], in1=ND[:, 1, fc], op=ALU.mult),
                              reads=[dND], partial=[dob[h]])
                    deferred.append(fin)
        while deferred:
            deferred.pop(0)()
        for tt in range(NTT):
            cs = slice(tt * TT, (tt + 1) * TT)
            for c in range(KC):
                pw, dpw = self.ps("mm")
                for h in range(KC):
                    kb.op("pe", lambda e: e.matmul(pw[:, :], lhsT=wo[:, h, c * 128:(c + 1) * 128], rhs=ob[:, h, cs],
                                                   start=(h == 0), stop=(h == KC - 1)),
                          reads=[dwo, dob[h]], partial=[dpw], flag=(h == KC - 1))
                kb.op("dve", lambda e: e.tensor_tensor(out=self.x[:, c, cs], in0=pw[:, :], in1=self.x[:, c, cs], op=ALU.add),
                      reads=[dpw], writes=[self.dx[c][tt]])

def prep_weights(inp):
    f = np.float32
    sm = np.zeros((128, NSMALL), f)

    def put(name, arr):
        sm[:, SM[name]:SM[name] + arr.shape[1]] = arr

    put("a_norm", inp["a_norm"].reshape(2, 8, 128).transpose(2, 0, 1).reshape(128, 16))
    put("f_norm", inp["f_norm"].reshape(4, 8, 128).transpose(2, 0, 1).reshape(128, 32))
    put("kv_norm", inp["kv_norm"].reshape(8, 128).T)
    put("b_norm", inp["b_norm"].reshape(2, 8, 128).transpose(2, 0, 1).reshape(128, 16))
    put("k_norm", inp["k_norm"].T)
    put("q_norm", inp["q_norm"].reshape(6, 128).T)
    put("o_norm", inp["a_out_norm"].reshape(2, 2, 128).transpose(2, 0, 1).reshape(128, 4))
    cv = np.concatenate([inp["f_conv"], inp["f_conv_b"][:, None, :]], axis=1)
    put("conv", cv.reshape(4, 4, 44, 128).transpose(3, 0, 2, 1).reshape(128, 4 * 44 * 4))
    w = {"small": sm, "cst": make_consts()}
    for l in range(2):
        wg = np.zeros((33, 512), f)
        wg[:16] = inp["a_w_gate"][l]
        wg[32] = inp["a_b_gate"][l]
        w[f"wg{l}"] = wg
        w[f"win{l}"] = np.ascontiguousarray(inp["a_w_in"][l].reshape(8, 128, INW).transpose(1, 0, 2))
        w[f"awo{l}"] = np.ascontiguousarray(inp["a_w_out"][l].reshape(8, 128, D).transpose(1, 0, 2))
        w[f"wq{l}"] = np.ascontiguousarray(inp["b_w_q"][l].reshape(8, 128, 24, 128).transpose(2, 1, 0, 3))
        w[f"bwo{l}"] = np.ascontiguousarray(inp["b_w_out"][l].reshape(8, 128, D).transpose(1, 0, 2))
    wkv = inp["w_kv"]
    w["wk"] = np.ascontiguousarray(wkv[:, :3072].reshape(8, 128, 24, 128).transpose(2, 1, 0, 3))
    w["wv"] = np.ascontiguousarray(wkv[:, 3072:].reshape(8, 128, 3, 1024).transpose(2, 1, 0, 3))
    for l in range(4):
        up = inp["f_w_up"][l].reshape(8, 128, 2, NJ, 128)
        w[f"wup{l}"] = np.ascontiguousarray(up.transpose(3, 1, 2, 0, 4)).reshape(NJ, 128, 2 * KC * 128)
        w[f"wdn{l}"] = np.ascontiguousarray(inp["f_w_down"][l].reshape(NJ, 128, D).transpose(1, 0, 2))
    return w


FULL_PHASES = [("gla", 0), ("ffn", 0), ("gla", 1), ("ffn", 1), ("kv",), ("dil", 0), ("ffn", 2), ("dil", 1), ("ffn", 3)]
_CACHE = {}
FUSED = True


LAST_EXEC_NS = [None]


def run_phases(phases, x, positions, w, nseq=NSEQ, trace=False):
    key = (tuple(phases), nseq)
    if key not in _CACHE:
        _CACHE[key] = Prog(phases, nseq)
    prog = _CACHE[key]
    B = x.shape[0]
    ncore = B // nseq
    xT = np.ascontiguousarray(x.reshape(ncore, nseq, S, KC, 128).transpose(0, 1, 4, 3, 2))
    posb = np.ascontiguousarray(np.broadcast_to(positions.reshape(ncore, nseq, 1, S), (ncore, nseq, 32, S))).astype(np.int32)
    in_maps = []
    for i in range(ncore):
        m = {"small": w["small"], "cst": w["cst"], "xT": xT[i]}
        for name in prog.inputs:
            m[name] = posb[i] if name == "posb" else w[name]
        in_maps.append(m)
    if trace:
        res = run_bass_kernel_spmd(prog.nc, in_maps, core_ids=list(range(ncore)), trace=True)
        LAST_EXEC_NS[0] = res.exec_time_ns
    else:
        res = run_bass_kernel_spmd(prog.nc, in_maps, core_ids=list(range(ncore)))
    yT = np.stack([r["yT"] for r in res.results])
    return np.ascontiguousarray(yT.transpose(0, 1, 4, 3, 2)).reshape(B, S, D)


def kernel(**inputs):
    inp = {k: np.asarray(v) for k, v in inputs.items()}
    w = prep_weights(inp)
    x = inp["x"].astype(np.float32, copy=False)
    if FUSED:
        return run_phases(FULL_PHASES, x, inp["positions"], w)
    x = run_phases(FULL_PHASES[:4], x, inp["positions"], w)
    return run_phases(FULL_PHASES[4:], x, inp["positions"], w)
```
